# Optimizing a Trainium2 kernel written in Bass

```python
import math
import jax, jax.numpy as jnp
from jax import lax
import numpy as np

D_MODEL = 1024
BATCH = 8
SEQ = 2048
DEPTH = 1
DEC_BATCH = 128
DEC_SEQ = 4
PAST_LEN = 16384
PAGE_SIZE = 128

POOL_WIDTH = D_MODEL // 2
POOL_WINDOWS = (2, 4, 8, 16)
POOL_GROUPS = len(POOL_WINDOWS)
POOL_GROUP_DIM = POOL_WIDTH // POOL_GROUPS
POOL_HIST = max(POOL_WINDOWS) - 1
SSM_WIDTH = D_MODEL // 2
SSM_GROUP_DIM = 16
SSM_GROUPS = SSM_WIDTH // SSM_GROUP_DIM
SSM_STATE = 64
DT_MIN = 0.001
DT_MAX = 0.1
N_BRANCHES = 2
IN_WIDTH = POOL_WIDTH + SSM_WIDTH + N_BRANCHES * D_MODEL
D_FF = -(-8 * D_MODEL // (3 * 256)) * 256
RMS_EPS = 1e-6

kernel_name = 'hybrid_pool_s5_gated_decoder_step'


def _rmsnorm(x, g):
    xf = x.astype(jnp.float32)
    y = xf * lax.rsqrt(jnp.mean(xf * xf, axis=-1, keepdims=True) + RMS_EPS)
    return (y * g.astype(jnp.float32)).astype(x.dtype)


def _pool_mixer(u_hist, u, start_pos, w_grp, scale):
    b, t, c = u.shape
    ext = jnp.concatenate([u_hist.astype(jnp.float32), u.astype(jnp.float32)], axis=1)
    cs = jnp.concatenate([jnp.zeros((b, 1, c), jnp.float32), jnp.cumsum(ext, axis=1)], axis=1)
    pos = start_pos + jnp.arange(t)
    base = POOL_HIST + 1
    pooled = []
    for g, w in enumerate(POOL_WINDOWS):
        lo_c, hi_c = g * POOL_GROUP_DIM, (g + 1) * POOL_GROUP_DIM
        win_sum = cs[:, base:base + t, lo_c:hi_c] - cs[:, base - w:base - w + t, lo_c:hi_c]
        cnt = jnp.minimum(w, pos + 1).astype(jnp.float32)[None, :, None]
        pooled.append(win_sum / cnt)
    pooled = jnp.stack(pooled, axis=2)
    diff = pooled - u.astype(jnp.float32).reshape(b, t, POOL_GROUPS, POOL_GROUP_DIM)
    mixed = jnp.einsum('btgc,gcd->btgd', diff, w_grp.astype(jnp.float32))
    return (mixed.reshape(b, t, c) * scale.astype(jnp.float32)).astype(u.dtype)


def _s5_discretise(a_re, a_im, log_dt):
    a_re = a_re.astype(jnp.float32)
    a_im = a_im.astype(jnp.float32)
    dt = jnp.exp(log_dt.astype(jnp.float32))[:, None]
    mag = jnp.exp(a_re * dt)
    ab_re = mag * jnp.cos(a_im * dt)
    ab_im = mag * jnp.sin(a_im * dt)
    num_re = ab_re - 1.0
    num_im = ab_im
    den = a_re * a_re + a_im * a_im
    coef_re = (num_re * a_re + num_im * a_im) / den
    coef_im = (num_im * a_re - num_re * a_im) / den
    return ab_re, ab_im, coef_re, coef_im


def _scan_combine(e1, e2):
    a1r, a1i, b1r, b1i = e1
    a2r, a2i, b2r, b2i = e2
    ar = a2r * a1r - a2i * a1i
    ai = a2r * a1i + a2i * a1r
    br = a2r * b1r - a2i * b1i + b2r
    bi = a2r * b1i + a2i * b1r + b2i
    return ar, ai, br, bi


def _s5_mixer(u, h_re, h_im, a_re, a_im, log_dt, b_re, b_im, c_re, c_im, d_skip):
    b, t, _ = u.shape
    uf = u.astype(jnp.float32)
    ug = uf.reshape(b, t, SSM_GROUPS, SSM_GROUP_DIM)
    ab_re, ab_im, coef_re, coef_im = _s5_discretise(a_re, a_im, log_dt)
    bu_re = jnp.einsum('btgh,gph->btgp', ug, b_re.astype(jnp.float32))
    bu_im = jnp.einsum('btgh,gph->btgp', ug, b_im.astype(jnp.float32))
    x_re = coef_re * bu_re - coef_im * bu_im
    x_im = coef_re * bu_im + coef_im * bu_re
    h_re = h_re.astype(jnp.float32)
    h_im = h_im.astype(jnp.float32)
    x_re = x_re.at[:, 0].add(ab_re * h_re - ab_im * h_im)
    x_im = x_im.at[:, 0].add(ab_re * h_im + ab_im * h_re)
    a_re_b = jnp.broadcast_to(ab_re, x_re.shape)
    a_im_b = jnp.broadcast_to(ab_im, x_im.shape)
    _, _, s_re, s_im = lax.associative_scan(_scan_combine, (a_re_b, a_im_b, x_re, x_im), axis=1)
    y = (jnp.einsum('btgp,ghp->btgh', s_re, c_re.astype(jnp.float32))
         - jnp.einsum('btgp,ghp->btgh', s_im, c_im.astype(jnp.float32)))
    y = y.reshape(b, t, SSM_WIDTH) + d_skip.astype(jnp.float32) * uf
    return y, s_re[:, -1], s_im[:, -1]


def _trunk(x, hist, st_re, st_im, start_pos, norm_mix, w_in, pool_w, pool_scale,
           ssm_a_re, ssm_a_im, ssm_log_dt, ssm_b_re, ssm_b_im, ssm_c_re, ssm_c_im, ssm_d,
           glu_w, glu_b, w_branch_pool, w_branch_ssm, w_out, norm_ffn,
           ffn_w_gate, ffn_w_up, ffn_w_down, norm_final):
    h = x
    b, t, _ = x.shape
    new_hist, new_re, new_im = [], [], []
    for l in range(DEPTH):
        xn = _rmsnorm(h, norm_mix[l])
        proj = jnp.einsum('btd,de->bte', xn, w_in[l])
        u_pool = proj[..., :POOL_WIDTH]
        u_ssm = proj[..., POOL_WIDTH:POOL_WIDTH + SSM_WIDTH]
        gates = jax.nn.sigmoid(proj[..., POOL_WIDTH + SSM_WIDTH:].astype(jnp.float32)
                               ).reshape(b, t, N_BRANCHES, D_MODEL)
        a_out = _pool_mixer(hist[l], u_pool, start_pos, pool_w[l], pool_scale[l])
        y_ssm, s_re_last, s_im_last = _s5_mixer(u_ssm, st_re[l], st_im[l], ssm_a_re[l], ssm_a_im[l],
                                                ssm_log_dt[l], ssm_b_re[l], ssm_b_im[l],
                                                ssm_c_re[l], ssm_c_im[l], ssm_d[l])
        z = jax.nn.gelu(y_ssm)
        b_out = (z * jax.nn.sigmoid(jnp.einsum('btc,ce->bte', z, glu_w[l].astype(jnp.float32))
                                    + glu_b[l].astype(jnp.float32))).astype(x.dtype)
        merged = (gates[:, :, 0] * jnp.einsum('btc,cd->btd', a_out, w_branch_pool[l]).astype(jnp.float32)
                  + gates[:, :, 1] * jnp.einsum('btc,cd->btd', b_out, w_branch_ssm[l]).astype(jnp.float32))
        h = h + jnp.einsum('btd,de->bte', merged.astype(x.dtype), w_out[l]).astype(h.dtype)
        hn = _rmsnorm(h, norm_ffn[l])
        f = jax.nn.silu(jnp.einsum('btd,df->btf', hn, ffn_w_gate[l])) * jnp.einsum('btd,df->btf', hn, ffn_w_up[l])
        h = h + jnp.einsum('btf,fd->btd', f, ffn_w_down[l]).astype(h.dtype)
        ext = jnp.concatenate([hist[l].astype(u_pool.dtype), u_pool], axis=1)
        new_hist.append(ext[:, -POOL_HIST:])
        new_re.append(s_re_last)
        new_im.append(s_im_last)
    y = _rmsnorm(h, norm_final)
    return y, jnp.stack(new_hist), jnp.stack(new_re), jnp.stack(new_im)


def setup_inputs(seed: int = 0) -> dict:
    key = jax.random.key(seed)
    ks = jax.random.split(key, 32)
    f32 = jnp.float32
    nrm = lambda k, shape, s: jax.random.normal(k, shape, f32) * s
    L = DEPTH
    a_re = -0.5 + nrm(ks[7], (L, SSM_GROUPS, SSM_STATE), 0.01)
    a_im = math.pi * jnp.arange(SSM_STATE, dtype=f32)[None, None, :] + nrm(ks[8], (L, SSM_GROUPS, SSM_STATE), 0.01)
    log_dt = jax.random.uniform(ks[9], (L, SSM_GROUPS), f32, math.log(DT_MIN), math.log(DT_MAX))
    return {
        'x_prompt': nrm(ks[0], (BATCH, SEQ, D_MODEL), 1.0),
        'x_sample': nrm(ks[1], (DEC_BATCH, DEC_SEQ, D_MODEL), 1.0),
        'state_pool': nrm(ks[2], (L, DEC_BATCH, POOL_HIST, POOL_WIDTH), 1.0),
        'state_ssm_re': nrm(ks[3], (L, DEC_BATCH, SSM_GROUPS, SSM_STATE), 0.5),
        'state_ssm_im': nrm(ks[4], (L, DEC_BATCH, SSM_GROUPS, SSM_STATE), 0.5),
        'norm_mix': 1.0 + nrm(ks[5], (L, D_MODEL), 0.02),
        'w_in': nrm(ks[6], (L, D_MODEL, IN_WIDTH), D_MODEL ** -0.5),
        'pool_w': nrm(ks[10], (L, POOL_GROUPS, POOL_GROUP_DIM, POOL_GROUP_DIM), POOL_GROUP_DIM ** -0.5),
        'pool_scale': 1.0 + nrm(ks[11], (L, POOL_WIDTH), 0.1),
        'ssm_a_re': a_re,
        'ssm_a_im': a_im,
        'ssm_log_dt': log_dt,
        'ssm_b_re': nrm(ks[12], (L, SSM_GROUPS, SSM_STATE, SSM_GROUP_DIM), (2 * SSM_GROUP_DIM) ** -0.5),
        'ssm_b_im': nrm(ks[13], (L, SSM_GROUPS, SSM_STATE, SSM_GROUP_DIM), (2 * SSM_GROUP_DIM) ** -0.5),
        'ssm_c_re': nrm(ks[14], (L, SSM_GROUPS, SSM_GROUP_DIM, SSM_STATE), (2 * SSM_STATE) ** -0.5),
        'ssm_c_im': nrm(ks[15], (L, SSM_GROUPS, SSM_GROUP_DIM, SSM_STATE), (2 * SSM_STATE) ** -0.5),
        'ssm_d': nrm(ks[16], (L, SSM_WIDTH), 1.0),
        'glu_w': nrm(ks[17], (L, SSM_WIDTH, SSM_WIDTH), SSM_WIDTH ** -0.5),
        'glu_b': nrm(ks[18], (L, SSM_WIDTH), 0.01),
        'w_branch_pool': nrm(ks[19], (L, POOL_WIDTH, D_MODEL), POOL_WIDTH ** -0.5),
        'w_branch_ssm': nrm(ks[20], (L, SSM_WIDTH, D_MODEL), SSM_WIDTH ** -0.5),
        'w_out': nrm(ks[21], (L, D_MODEL, D_MODEL), D_MODEL ** -0.5),
        'norm_ffn': 1.0 + nrm(ks[22], (L, D_MODEL), 0.02),
        'ffn_w_gate': nrm(ks[23], (L, D_MODEL, D_FF), D_MODEL ** -0.5),
        'ffn_w_up': nrm(ks[24], (L, D_MODEL, D_FF), D_MODEL ** -0.5),
        'ffn_w_down': nrm(ks[25], (L, D_FF, D_MODEL), D_FF ** -0.5),
        'norm_final': 1.0 + nrm(ks[26], (D_MODEL,), 0.02),
    }


def reference(x_prompt, x_sample, state_pool, state_ssm_re, state_ssm_im,
              norm_mix, w_in, pool_w, pool_scale, ssm_a_re, ssm_a_im, ssm_log_dt,
              ssm_b_re, ssm_b_im, ssm_c_re, ssm_c_im, ssm_d, glu_w, glu_b,
              w_branch_pool, w_branch_ssm, w_out, norm_ffn, ffn_w_gate, ffn_w_up,
              ffn_w_down, norm_final):
    bp = x_prompt.shape[0]
    zero_hist = jnp.zeros((DEPTH, bp, POOL_HIST, POOL_WIDTH), x_prompt.dtype)
    zero_ssm = jnp.zeros((DEPTH, bp, SSM_GROUPS, SSM_STATE), jnp.float32)
    y_prompt, pool_p, re_p, im_p = _trunk(
        x_prompt, zero_hist, zero_ssm, zero_ssm, 0,
        norm_mix, w_in, pool_w, pool_scale, ssm_a_re, ssm_a_im, ssm_log_dt,
        ssm_b_re, ssm_b_im, ssm_c_re, ssm_c_im, ssm_d, glu_w, glu_b,
        w_branch_pool, w_branch_ssm, w_out, norm_ffn, ffn_w_gate, ffn_w_up,
        ffn_w_down, norm_final)
    y_sample, pool_s, re_s, im_s = _trunk(
        x_sample, state_pool, state_ssm_re, state_ssm_im, PAST_LEN,
        norm_mix, w_in, pool_w, pool_scale, ssm_a_re, ssm_a_im, ssm_log_dt,
        ssm_b_re, ssm_b_im, ssm_c_re, ssm_c_im, ssm_d, glu_w, glu_b,
        w_branch_pool, w_branch_ssm, w_out, norm_ffn, ffn_w_gate, ffn_w_up,
        ffn_w_down, norm_final)
    return (y_prompt, y_sample, pool_p, re_p, im_p, pool_s, re_s, im_s)
```

```python
import math
import numpy as np
import concourse.bass as bass
import concourse.mybir as mybir
from concourse.bass_utils import run_bass_kernel_spmd

F32 = mybir.dt.float32
BF16 = mybir.dt.bfloat16
I32 = mybir.dt.int32
ALU = mybir.AluOpType
AF = mybir.ActivationFunctionType

N_CORES = 8
D = 1024
SEQ = 2048
DFF = 2816
NFC = DFF // 128
NS = 16
TS = 4
HIST = 15
EPS = 1e-6
LP = 8
LS = 4
LMAX = 8
TP = 512
COMPUTE = ("pe", "act", "dve", "pool")
N_DSEM = 8
DEBUG = {}


class Op:
    __slots__ = ("eng", "emit", "waits", "tok", "is_dma", "name")


class Sched:
    def __init__(self, nc):
        self.nc = nc
        self.ops = {e: [] for e in COMPUTE + ("sp",)}
        self.sem = {e: nc.alloc_semaphore("sem_" + e) for e in COMPUTE}
        self.cnt = {e: 0 for e in COMPUTE}
        self.dsem = {q: [nc.alloc_semaphore(f"dsem_{q}{i}") for i in range(N_DSEM)]
                     for q in ("sp", "pool", "act")}
        self.dval = {q: [0] * N_DSEM for q in ("sp", "pool", "act")}
        self.dnext = {q: 0 for q in ("sp", "pool", "act")}
        self.last_w = {}
        self.readers = {}
        self.pending = {}
        self.known = {e: {} for e in COMPUTE + ("sp",)}
        self.nops = 0

    @staticmethod
    def _add(waits, tok):
        if tok is None:
            return
        s, v = tok
        k = id(s)
        if k not in waits or waits[k][1] < v:
            waits[k] = (s, v)

    def tokens_of(self, keys):
        out = []
        for k in keys:
            if k in self.last_w:
                out.append(self.last_w[k])
            out.extend(self.readers.get(k, ()))
            out.extend(self.pending.get(k, ()))
        return out

    def alias(self, new_keys, old_keys):
        toks = self.tokens_of(old_keys)
        for k in new_keys:
            self.pending.setdefault(k, []).extend(toks)

    def op(self, eng, emit, reads=(), writes=(), deps=(), dma=False, name=""):
        o = Op()
        o.eng, o.emit, o.is_dma, o.name = eng, emit, dma, name
        waits = {}
        for r in reads:
            self._add(waits, self.last_w.get(r))
        for w in writes:
            self._add(waits, self.last_w.get(w))
            for t in self.readers.get(w, ()):
                self._add(waits, t)
            for t in self.pending.pop(w, ()):
                self._add(waits, t)
        for d in deps:
            self._add(waits, d)
        if dma:
            j = self.dnext[eng]
            self.dnext[eng] = (j + 1) % N_DSEM
            s = self.dsem[eng][j]
            if self.dval[eng][j] > 0:
                self._add(waits, (s, self.dval[eng][j]))
            self.dval[eng][j] += 16
            o.tok = (s, self.dval[eng][j])
        else:
            self.cnt[eng] += 1
            o.tok = (self.sem[eng], self.cnt[eng])
        known = self.known[eng]
        wl = []
        for k, (s, v) in waits.items():
            if eng == "pe" and s is self.sem["pe"]:
                continue
            if known.get(k, 0) >= v:
                continue
            known[k] = v
            wl.append((s, v))
        o.waits = wl
        for r in reads:
            self.readers.setdefault(r, []).append(o.tok)
        for w in writes:
            self.last_w[w] = o.tok
            self.readers[w] = []
        self.ops[eng].append(o)
        self.nops += 1
        return o.tok

    def _replay(self, name, e):
        for o in self.ops[name]:
            for s, v in o.waits:
                e.wait_ge(s, v)
            ins = o.emit(e)
            ins.then_inc(o.tok[0], 16 if o.is_dma else 1)

    def emit_all(self, final_tokens):
        nc = self.nc
        best = {}
        for s, v in final_tokens:
            if id(s) not in best or best[id(s)][1] < v:
                best[id(s)] = (s, v)
        with nc.Block() as block:
            @block.tensor
            def _(e):
                self._replay("pe", e)

            @block.scalar
            def _(e):
                self._replay("act", e)

            @block.vector
            def _(e):
                self._replay("dve", e)

            @block.gpsimd
            def _(e):
                self._replay("pool", e)

            @block.sync
            def _(e):
                self._replay("sp", e)
                for s, v in best.values():
                    e.wait_ge(s, v)


def bc(ap, shape):
    return ap.broadcast_to(list(shape))


class Builder:
    def __init__(self, debug=(), stream_order=None, i_last0=None):
        self.debug = set(debug)
        self.stream_order = stream_order
        self.i_last0 = i_last0
        self.slot_last = {}
        self.sample_first_idx = None
        self.rec = []
        self.stream_pos = 0
        self.stream_issued = 0
        nc = self.nc = bass.Bass("TRN2", target_bir_lowering=False)
        self.S = Sched(nc)
        self.final = []
        self.dbg_out = {}
        self._uid = 0
        self.declare_io()
        self.alloc()

    def din(self, name, shape, dt=F32):
        return self.nc.dram_tensor(name, list(shape), dt, kind="ExternalInput")

    def dout(self, name, shape, dt=F32):
        return self.nc.dram_tensor(name, list(shape), dt, kind="ExternalOutput")

    def dscr(self, name, shape, dt=BF16):
        return self.nc.dram_tensor(name, list(shape), dt, kind="Internal")

    def sb(self, name, shape, dt=F32):
        return self.nc.alloc_sbuf_tensor(name, list(shape), dt).ap()

    def uid(self, p="k"):
        self._uid += 1
        return f"{p}{self._uid}"

    def dve(self, fn, reads=(), writes=(), **kw):
        return self.S.op("dve", fn, reads, writes, **kw)

    def act(self, fn, reads=(), writes=(), **kw):
        return self.S.op("act", fn, reads, writes, **kw)

    def pool(self, fn, reads=(), writes=(), **kw):
        return self.S.op("pool", fn, reads, writes, **kw)

    def pe(self, fn, reads=(), writes=(), **kw):
        return self.S.op("pe", fn, reads, writes, **kw)

    def load(self, out, in_, reads=(), writes=(), slow=False, q="sp", **kw):
        if slow:
            fn = lambda e: e.dma_start(out=out, in_=in_, allow_slow_non_contiguous=True)
        else:
            fn = lambda e: e.dma_start(out=out, in_=in_)
        return self.S.op(q, fn, reads, writes, dma=True, **kw)

    def store(self, out, in_, reads=(), writes=(), final=True, **kw):
        t = self.S.op("pool", lambda e: e.dma_start(out=out, in_=in_), reads, writes, dma=True, **kw)
        if final:
            self.final.append(t)
        return t

    def dbg(self, name, ap, key, shape):
        if name not in self.debug:
            return
        o = self.dout("dbg_" + name, shape, ap.dtype)
        self.dbg_out[name] = o
        self.store(o.ap(), ap, reads=[key])

    def declare_io(self):
        d = self.din
        self.xp = d("xp", [SEQ, D])
        self.xs = d("xs", [NS * TS, D])
        self.spool = d("spool", [NS, HIST, 512])
        self.sre = d("sre", [NS, 2048])
        self.sim = d("sim", [NS, 2048])
        self.norm_mix = d("norm_mix", [D])
        self.w_in = d("w_in", [D, 3072])
        self.pool_w = d("pool_w", [4, 128, 128])
        self.pool_scale = d("pool_scale", [512])
        self.a_re = d("ssm_a_re", [32 * 64])
        self.a_im = d("ssm_a_im", [32 * 64])
        self.log_dt = d("ssm_log_dt", [32])
        self.b_re = d("ssm_b_re", [32 * 64 * 16])
        self.b_im = d("ssm_b_im", [32 * 64 * 16])
        self.c_re = d("ssm_c_re", [32 * 16 * 64])
        self.c_im = d("ssm_c_im", [32 * 16 * 64])
        self.ssm_d = d("ssm_d", [512])
        self.glu_w = d("glu_w", [512, 512])
        self.glu_b = d("glu_b", [512])
        self.wbp = d("w_branch_pool", [512, D])
        self.wbs = d("w_branch_ssm", [512, D])
        self.w_out = d("w_out", [D, D])
        self.norm_ffn = d("norm_ffn", [D])
        self.wg = d("ffn_w_gate", [D, DFF])
        self.wu = d("ffn_w_up", [D, DFF])
        self.wd = d("ffn_w_down", [DFF, D])
        self.norm_final = d("norm_final", [D])
        o = self.dout
        self.yp = o("yp", [SEQ, D])
        self.ys = o("ys", [NS * TS, D])
        self.o_pool_p = o("o_pool_p", [HIST, 512])
        self.o_re_p = o("o_re_p", [16, 128])
        self.o_im_p = o("o_im_p", [16, 128])
        self.o_pool_s = o("o_pool_s", [NS, HIST, 512])
        self.o_re_s = o("o_re_s", [NS, 2048])
        self.o_im_s = o("o_im_s", [NS, 2048])
        s = self.dscr
        self.s_win = s("s_win", [6, 128, 8, 512])
        self.s_pw = s("s_pw", [128, 4, 128])
        self.s_glu = s("s_glu", [128, 4, 512])
        self.s_wbp = s("s_wbp", [128, 4, 1024])
        self.s_wbs = s("s_wbs", [128, 4, 1024])
        self.s_wout = s("s_wout", [2, 128, 8, 512])
        self.s_wg = s("s_wg", [11, 128, 8, 256])
        self.s_wu = s("s_wu", [11, 128, 8, 256])
        self.s_wd = s("s_wd", [NFC, 128, 1024])

    def alloc(self):
        sb = self.sb
        nc = self.nc
        self.ps = [nc.alloc_psum_tensor(f"ps{i}", [128, 512], F32).ap() for i in range(8)]
        self.psk = [[f"ps{i}"] for i in range(8)]
        self.ps_rrd = {}
        self.bpool = "all"
        self.ident_f = sb("ident_f", [128, 128])
        self.ident_b = sb("ident_b", [128, 128], BF16)
        self.maskS = sb("maskS", [128, 2])
        self.maskQ = sb("maskQ", [128, 4])
        self.halfpi = sb("halfpi", [128, 1])
        self.invc = sb("invc", [128, 16])
        self.gm = sb("gm", [128, 8])
        self.gn = sb("gn", [128, 8])
        self.gf = sb("gf", [128, D])
        self.pscale = sb("pscale", [128, 4])
        self.dsk = sb("dsk", [128, 4])
        self.hgb = sb("hgb", [128, 4])
        self.pw = sb("pw", [128, 4, 128], BF16)
        self.glu = sb("glu", [128, 4, 512], BF16)
        self.wbp_sb = sb("wbp_sb", [128, 4, 1024], BF16)
        self.wbs_sb = sb("wbs_sb", [128, 4, 1024], BF16)
        self.XW = sb("XW", [128, 4, LMAX, 2, 128], BF16)
        self.CW = sb("CW", [128, 16, LMAX, 2, 32], BF16)
        self.KW = sb("KW", [128, 4, LMAX, 128], BF16)
        self.Mrr = {L: sb(f"Mrr{L}", [128, 2, 16]) for L in (LS, LP)}
        self.Mii = {L: sb(f"Mii{L}", [128, 2, 16]) for L in (LS, LP)}
        self.NRING = 3
        self.ring = [sb(f"ring{i}", [128, 4096], BF16) for i in range(self.NRING)]
        self.xt = [sb(f"xt{i}", [128, 4, D]) for i in range(2)]
        x1 = self.xt[1].rearrange("p a b -> p (a b)").bitcast(BF16)
        self.ring += [x1[:, 0:4096], x1[:, 4096:8192]]
        self.ringk = {0: ["ring0a", "ring0b"], 1: ["ring1a", "ring1b"], 2: ["ring2a", "ring2b"],
                      3: ["xt1_0", "xt1_1"], 4: ["xt1_2", "xt1_3"]}
        self.xb = [[sb(f"xb{s}{i}", [128, D], BF16) for i in range(2 - s)] for s in range(2)]
        self.ss = [sb(f"ss{s}", [128, 4]) for s in range(2)]
        self.rs = [sb(f"rs{s}", [128, 4]) for s in range(2)]
        self.rtmp = [[sb(f"rtmp{s}{i}", [128, 4]) for i in range(3)] for s in range(2)]
        self.xnT = sb("xnT", [128, 8, TP], BF16)
        self.hnT = sb("hnT", [128, 8, TP], BF16)
        self.arB = sb("arB", [128, 6144 + NFC * 512], BF16)
        self.fA = sb("fA", [128, 2112])
        self.fB = sb("fB", [128, 1056])
        self.fX = sb("fX", [128, 2048])
        self.fC = sb("fC", [128, 3, TP])
        self.stash = sb("stash", [128, 4, HIST])
        self.carry = sb("carry", [128, 2, 16])
        self.sct = [sb(f"sct{i}", [128, 2, 16]) for i in range(4)]
        self.msm = sb("msm", [128, 1, 128])
        xf = self.xt[0].rearrange("p a b -> p (a b)")
        self.sst = [xf[0:NS, 1024:3072], xf[0:NS, 1024:3072]]
        self.sstk = [["xt0_1", "xt0_2"], ["xt0_1", "xt0_2"]]

    def bank(self):
        pool = {"A": [4, 5], "B": [0, 1, 2, 3], "all": [0, 1, 2, 3, 4, 5]}[self.bpool]
        i = self.ps_rrd.get(self.bpool, 0)
        self.ps_rrd[self.bpool] = (i + 1) % len(pool)
        return pool[i]

    def setup_consts(self):
        P = self.pool
        D_ = self.dve
        idf, idb = self.ident_f, self.ident_b
        P(lambda e: e.memset(idf, 1.0), writes=["ident_f"])
        P(lambda e: e.affine_select(out=idf, in_=idf, compare_op=ALU.is_equal, fill=0.0, base=0,
                                    pattern=[[-1, 128]], channel_multiplier=1),
          reads=["ident_f"], writes=["ident_f"])
        D_(lambda e: e.tensor_copy(out=idb, in_=idf), reads=["ident_f"], writes=["ident_b"])
        mS, mQ = self.maskS, self.maskQ
        P(lambda e: e.memset(mS, 0.0), writes=["maskS"])
        P(lambda e: e.memset(mS[0:64, 0:1], 1.0), writes=["maskS"])
        P(lambda e: e.memset(mS[64:128, 1:2], 1.0), writes=["maskS"])
        P(lambda e: e.memset(mQ, 0.0), writes=["maskQ"])
        for q in range(4):
            P(lambda e, q=q: e.memset(mQ[32 * q:32 * q + 32, q:q + 1], 1.0), writes=["maskQ"])
        P(lambda e: e.memset(self.halfpi, math.pi / 2), writes=["halfpi"])
        P(lambda e: e.iota(self.invc, [[1, 16]], base=1, channel_multiplier=0,
                           allow_small_or_imprecise_dtypes=True), writes=["invc"])
        D_(lambda e: e.reciprocal(out=self.invc, in_=self.invc), reads=["invc"], writes=["invc"])
        P(lambda e: e.memset(self.stash, 0.0), writes=["stash"])
        P(lambda e: e.memset(self.carry, 0.0), writes=["carry"])
        L = self.load

    def load_small_vectors(self):
        L = self.load
        D_ = self.dve
        L(self.gn, self.norm_ffn.ap().rearrange("(k p) -> p k", p=128), writes=["gn"], slow=True, q="act")
        L(self.gf, bass.AP(self.norm_final, 0, [[0, 128], [1, D]]), writes=["gf"])
        L(self.pscale, self.pool_scale.ap().rearrange("(k p) -> p k", p=128), writes=["pscale"], slow=True, q="act")
        L(self.dsk, self.ssm_d.ap().rearrange("(k p) -> p k", p=128), writes=["dsk"], slow=True, q="act")
        L(self.hgb, self.glu_b.ap().rearrange("(k p) -> p k", p=128), writes=["hgb"], slow=True, q="act")
        D_(lambda e: e.tensor_scalar(out=self.hgb, in0=self.hgb, scalar1=0.5, scalar2=None, op0=ALU.mult),
           reads=["hgb"], writes=["hgb"])


    def convert_weights(self):
        def cv(out, in_, key):
            self.S.op("pool", lambda e: e.dma_start(out=out, in_=in_), writes=[key], dma=True)
        win = self.w_in.ap().rearrange("(k p) n -> p k n", p=128)
        for c in range(6):
            cv(self.s_win.ap()[c], win[:, :, c * 512:(c + 1) * 512], f"s_win{c}")
        cv(self.s_pw.ap(), self.pool_w.ap().rearrange("g c d -> c g d"), "s_pw")
        cv(self.s_glu.ap(), self.glu_w.ap().rearrange("(k p) n -> p k n", p=128), "s_glu")
        cv(self.s_wbp.ap(), self.wbp.ap().rearrange("(k p) n -> p k n", p=128), "s_wbp")
        cv(self.s_wbs.ap(), self.wbs.ap().rearrange("(k p) n -> p k n", p=128), "s_wbs")
        wo = self.w_out.ap().rearrange("(k p) n -> p k n", p=128)
        for h in range(2):
            cv(self.s_wout.ap()[h], wo[:, :, h * 512:(h + 1) * 512], f"s_wout{h}")

    def convert_weights_ffn(self):
        def cv(out, in_, key):
            self.S.op("pool", lambda e: e.dma_start(out=out, in_=in_), writes=[key], dma=True)
        wg = self.wg.ap().rearrange("(k p) n -> p k n", p=128)
        wu = self.wu.ap().rearrange("(k p) n -> p k n", p=128)
        for j in range(11):
            cv(self.s_wg.ap()[j], wg[:, :, j * 256:(j + 1) * 256], f"s_wg{j}")
            cv(self.s_wu.ap()[j], wu[:, :, j * 256:(j + 1) * 256], f"s_wu{j}")
            yield
        wd = self.wd.ap().rearrange("(f p) n -> f p n", p=128)
        for f0 in range(0, NFC, 2):
            cv(self.s_wd.ap()[f0:f0 + 2], wd[f0:f0 + 2], f"s_wd{f0 // 2}")
            if f0 % 4 == 2:
                yield

    def load_resident(self):
        self.load(self.pw, self.s_pw.ap(), reads=["s_pw"], writes=["pw"])
        self.load(self.glu, self.s_glu.ap(), reads=["s_glu"], writes=["glu"])
        self.load(self.wbp_sb, self.s_wbp.ap(), reads=["s_wbp"], writes=["wbp"])
        self.load(self.wbs_sb, self.s_wbs.ap(), reads=["s_wbs"], writes=["wbs"])

    def wspec(self, name):
        if name.startswith("win"):
            c = int(name[3:])
            return [(lambda r: r.rearrange("p (k n) -> p k n", k=8), self.s_win.ap()[c], [f"s_win{c}"])]
        if name.startswith("wout"):
            h = int(name[4:])
            return [(lambda r: r.rearrange("p (k n) -> p k n", k=8), self.s_wout.ap()[h], [f"s_wout{h}"])]
        if name.startswith("gu"):
            j = int(name[2:])
            return [(lambda r: r[:, 0:2048].rearrange("p (k n) -> p k n", k=8), self.s_wg.ap()[j], [f"s_wg{j}"]),
                    (lambda r: r[:, 2048:4096].rearrange("p (k n) -> p k n", k=8), self.s_wu.ap()[j], [f"s_wu{j}"])]
        if name.startswith("wd"):
            f0 = int(name[2:])
            nf = min(4, NFC - f0)
            return [(lambda r, nf=nf: r[:, 0:nf * 1024].rearrange("p (f n) -> p f n", f=nf),
                     self.s_wd.ap()[f0:f0 + nf].rearrange("f p n -> p f n"),
                     [f"s_wd{(f0 + i) // 2}" for i in range(0, nf, 2)])]
        raise KeyError(name)

    def slot_of(self, i):
        if self.i_last0 is None or i < self.i_last0 + 3:
            return i % self.NRING
        return [3, 4, 0, 1, 2][(i - self.i_last0 - 3) % 5]

    def prefetch(self, upto):
        if self.stream_order is None:
            return
        upto = min(upto, len(self.stream_order))
        consumed = self.stream_pos - 1
        while self.stream_issued < upto:
            i = self.stream_issued
            slot = self.slot_of(i)
            prev = self.slot_last.get(slot, -1)
            if prev >= 0 and prev >= consumed:
                break
            parts = self.wspec(self.stream_order[i])
            ks = self.ringk[slot]
            for pi, (vf, src, skeys) in enumerate(parts):
                wk = ks if len(parts) == 1 else [ks[pi]]
                self.load(vf(self.ring[slot]), src, reads=skeys, writes=wk)
            self.slot_last[slot] = i
            self.stream_issued += 1

    def next_w(self, expect, nopref=False):
        i = self.stream_pos
        self.stream_pos += 1
        if self.stream_order is None:
            self.rec.append(expect)
        else:
            assert self.stream_order[i] == expect, (i, self.stream_order[i], expect)
            depth = 5 if (self.i_last0 is not None and i >= self.i_last0) else self.NRING
            if not nopref:
                self.prefetch(i + depth)
            assert self.stream_issued > i, (i, self.stream_issued)
        slot = self.slot_of(i)
        return self.ring[slot], list(self.ringk[slot])

    def scr(self, name, shape, pool=1, dt=F32):
        n = int(np.prod(shape[1:]))
        if dt == BF16:
            n = (n + 1) // 2
        st = self.scr_state[pool]
        while True:
            buf = st["bufs"][st["i"]]
            if st["off"] + n <= buf.shape[1]:
                break
            st["i"] += 1
            st["off"] = 0
        ap = buf[:, st["off"]:st["off"] + n]
        st["off"] += n
        self.setup_keys[pool].append(name)
        if dt == BF16:
            ap = ap.bitcast(BF16)[:, 0:int(np.prod(shape[1:]))]
        if len(shape) == 3:
            ap = ap.rearrange("p (a b) -> p a b", b=shape[2])
        return ap

    def cmul(self, eng, ore, oim, are, aim, bre, bim, t1, t2, rk, wk, tk):
        E = lambda fn, r, w: self.S.op(eng, fn, r, w)
        TT = lambda o, a, b, op: (lambda e: e.tensor_tensor(out=o, in0=a, in1=b, op=op))
        E(TT(t1, are, bre, ALU.mult), rk, [tk + "1"])
        E(TT(t2, aim, bim, ALU.mult), rk, [tk + "2"])
        E(TT(ore, t1, t2, ALU.subtract), [tk + "1", tk + "2"], wk)
        E(TT(t1, are, bim, ALU.mult), rk, [tk + "1"])
        E(TT(t2, aim, bre, ALU.mult), rk, [tk + "2"])
        E(TT(oim, t1, t2, ALU.add), [tk + "1", tk + "2"], wk)

    def discretise(self, are, aim, dtb):
        names = ("x1", "mag", "ang", "nn", "rr", "sn", "cs", "ar", "abre", "abim", "cre", "cim", "t1", "t2", "den", "nre")
        T = {n: self.scr("S_" + n, [128, 16]) for n in names}
        K = lambda s: "S_" + s
        E = lambda fn, r, w: self.S.op("dve", fn, [K(x) for x in r], [K(x) for x in w])
        A = lambda fn, r, w: self.S.op("act", fn, [K(x) for x in r], [K(x) for x in w])
        TT = lambda o, a, b, op: (lambda e: e.tensor_tensor(out=T[o] if isinstance(o, str) else o,
                                                            in0=T[a] if isinstance(a, str) else a,
                                                            in1=T[b] if isinstance(b, str) else b, op=op))
        E(TT("x1", are, dtb, ALU.mult), ["are", "dt"], ["x1"])
        A(lambda e: e.activation(out=T["mag"], in_=T["x1"], func=AF.Exp), ["x1"], ["mag"])
        E(TT("ang", aim, dtb, ALU.mult), ["aim", "dt"], ["ang"])
        E(lambda e: e.tensor_scalar(out=T["nn"], in0=T["ang"], scalar1=math.pi, scalar2=None, op0=ALU.is_gt), ["ang"], ["nn"])
        for m in range(2, 9):
            E(lambda e, m=m: e.scalar_tensor_tensor(out=T["nn"], in0=T["ang"], scalar=(2 * m - 1) * math.pi, in1=T["nn"],
                                                    op0=ALU.is_gt, op1=ALU.add), ["ang", "nn"], ["nn"])
        E(lambda e: e.scalar_tensor_tensor(out=T["rr"], in0=T["nn"], scalar=-2.0 * math.pi, in1=T["ang"],
                                           op0=ALU.mult, op1=ALU.add), ["nn", "ang"], ["rr"])
        A(lambda e: e.activation(out=T["sn"], in_=T["rr"], func=AF.Sin), ["rr"], ["sn"])
        E(lambda e: e.tensor_scalar(out=T["ar"], in0=T["rr"], scalar1=-1.0, scalar2=None, op0=ALU.mult), ["rr"], ["ar"])
        E(TT("ar", "ar", "rr", ALU.max), ["ar", "rr"], ["ar"])
        self.S.op("act", lambda e: e.activation(out=T["cs"], in_=T["ar"], func=AF.Sin, scale=-1.0, bias=self.halfpi[:, 0:1]),
                  ["S_ar", "halfpi"], ["S_cs"])
        E(TT("abre", "mag", "cs", ALU.mult), ["mag", "cs"], ["abre"])
        E(TT("abim", "mag", "sn", ALU.mult), ["mag", "sn"], ["abim"])
        E(TT("t1", are, are, ALU.mult), ["are"], ["t1"])
        E(TT("t2", aim, aim, ALU.mult), ["aim"], ["t2"])
        E(TT("den", "t1", "t2", ALU.add), ["t1", "t2"], ["den"])
        E(lambda e: e.reciprocal(out=T["den"], in_=T["den"]), ["den"], ["den"])
        E(lambda e: e.tensor_scalar(out=T["nre"], in0=T["abre"], scalar1=-1.0, scalar2=None, op0=ALU.add), ["abre"], ["nre"])
        E(TT("t1", "nre", are, ALU.mult), ["nre", "are"], ["t1"])
        E(TT("t2", "abim", aim, ALU.mult), ["abim", "aim"], ["t2"])
        E(TT("t1", "t1", "t2", ALU.add), ["t1", "t2"], ["t1"])
        E(TT("cre", "t1", "den", ALU.mult), ["t1", "den"], ["cre"])
        E(TT("t1", "abim", are, ALU.mult), ["abim", "are"], ["t1"])
        E(TT("t2", "nre", aim, ALU.mult), ["nre", "aim"], ["t2"])
        E(TT("t1", "t1", "t2", ALU.subtract), ["t1", "t2"], ["t1"])
        E(TT("cim", "t1", "den", ALU.mult), ["t1", "den"], ["cim"])
        return T["abre"], T["abim"], T["cre"], T["cim"]

    def setup_s5_p1(self):
        S = self.S
        L = self.load
        scr = self.scr
        flat = lambda x: x.rearrange("p a b -> p (a b)")
        self.scr_state = {
            1: {"bufs": [self.fA, self.fX, self.fB, self.arB[:, 0:6144].bitcast(F32)], "i": 0, "off": 0},
            2: {"bufs": [flat(self.xt[1]), self.arB[:, 6144:6144 + NFC * 512].bitcast(F32), flat(self.hnT).bitcast(F32),
                         flat(self.fC)], "i": 0, "off": 0},
        }
        self.setup_keys = {1: [], 2: []}
        prm = scr("S_prm", [128, 3, 16])
        areS, aimS, ldtS = prm[:, 0, :], prm[:, 1, :], prm[:, 2, :]
        dtS = scr("S_dt", [128, 16])
        stg = scr("S_stg", [128, 384])
        ld2 = scr("S_ld2", [128, 2])
        L(stg[0:16, 0:128], bass.AP(self.a_re, 0, [[128, 16], [1, 128]]), writes=["S_stg0"])
        L(stg[0:16, 128:256], bass.AP(self.a_im, 0, [[128, 16], [1, 128]]), writes=["S_stg1"], q="act")
        L(ld2[0:16, :], bass.AP(self.log_dt, 0, [[2, 16], [1, 2]]), writes=["S_ld2"])
        self.setup_keys[1] += ["S_stg0", "S_stg1", "S_stg2"]
        self.dve(lambda e: e.tensor_copy(out=stg[0:16, 256:384].rearrange("p (a c) -> p a c", a=2),
                                         in_=bc(ld2[0:16, :].unsqueeze(2), [16, 2, 64])), ["S_ld2"], ["S_stg2"])
        b = self.bank()
        def trp(e, b=b):
            for j in range(3):
                ins = e.transpose(out=self.ps[b][:, j * 16:(j + 1) * 16], in_=stg[0:16, j * 128:(j + 1) * 128],
                                  identity=self.ident_f[0:16, 0:16])
            return ins
        self.pe(trp, ["S_stg0", "S_stg1", "S_stg2", "ident_f"], self.psk[b])
        self.dve(lambda e, b=b: e.tensor_copy(out=prm.rearrange("p j s -> p (j s)"), in_=self.ps[b][:, 0:48]),
                 self.psk[b], ["S_are", "S_aim", "S_ldt"])
        self.setup_keys[1] += ["S_are", "S_aim", "S_ldt"]
        self.act(lambda e: e.activation(out=dtS, in_=ldtS, func=AF.Exp), ["S_ldt"], ["S_dt"])
        abreS, abimS, creS, cimS = self.discretise(areS, aimS, dtS)
        PS_re, PS_im = scr("PS_re", [128, 9, 16], 2), scr("PS_im", [128, 9, 16], 2)
        self.PS_re, self.PS_im = PS_re, PS_im
        tS1, tS2 = scr("tS1", [128, 16]), scr("tS2", [128, 16])
        self.setup_keys[2] += [f"PSre{k}" for k in range(9)] + [f"PSim{k}" for k in range(9)]
        self.dve(lambda e: e.tensor_copy(out=PS_re[:, 1, :], in_=abreS), ["S_abre"], ["PSre1"])
        self.dve(lambda e: e.tensor_copy(out=PS_im[:, 1, :], in_=abimS), ["S_abim"], ["PSim1"])
        for k in range(1, 8):
            self.cmul("dve", PS_re[:, k + 1, :], PS_im[:, k + 1, :], PS_re[:, k, :], PS_im[:, k, :], abreS, abimS,
                      tS1, tS2, [f"PSre{k}", f"PSim{k}", "S_abre", "S_abim"], [f"PSre{k + 1}", f"PSim{k + 1}"], "tS")
        for L_ in (LS, LP):
            mrr, mii = self.Mrr[L_], self.Mii[L_]
            mk = f"M{L_}"
            self.dve(lambda e, mrr=mrr, L_=L_: e.tensor_copy(out=mrr, in_=bc(PS_re[:, L_:L_ + 1, :], [128, 2, 16])),
                     [f"PSre{L_}"], [mk])
            self.dve(lambda e, mii=mii, L_=L_: e.tensor_scalar(out=mii[:, 0, :], in0=PS_im[:, L_, :], scalar1=-1.0,
                                                              scalar2=None, op0=ALU.mult), [f"PSim{L_}", mk], [mk])
            self.dve(lambda e, mii=mii, L_=L_: e.tensor_copy(out=mii[:, 1, :], in_=PS_im[:, L_, :]),
                     [f"PSim{L_}", mk], [mk])
        BSr, BSi = scr("BSr", [128, 16, 16]), scr("BSi", [128, 16, 16])
        L(BSr, bass.AP(self.b_re, 0, [[16, 128], [2048, 16], [1, 16]]), writes=["BSr"])
        L(BSi, bass.AP(self.b_im, 0, [[16, 128], [2048, 16], [1, 16]]), writes=["BSi"])
        Bk = [[scr(f"Bk{i}r", [128, 16, 16]), scr(f"Bk{i}i", [128, 16, 16])] for i in range(2)]
        tB1, tB2 = scr("tB1", [128, 16, 16]), scr("tB2", [128, 16, 16])
        cb = lambda x: bc(x.unsqueeze(2), [128, 16, 16])
        self.cmul("dve", Bk[0][0], Bk[0][1], cb(creS), cb(cimS), BSr, BSi, tB1, tB2,
                  ["S_cre", "S_cim", "BSr", "BSi"], ["Bk0r", "Bk0i"], "tB")
        self.ZB = ZB = [scr("ZBr", [128, 16, 128], 2, BF16), scr("ZBi", [128, 16, 128], 2, BF16)]
        for ri in range(2):
            zk = "ZBr" if ri == 0 else "ZBi"
            bk = "Bk0r" if ri == 0 else "Bk0i"
            Z = ZB[ri]
            self.dve(lambda e, Z=Z: e.memset(Z, 0.0), [], [zk])
            Zv = Z.rearrange("p (f q) (u a c) -> p f q u a c", q=4, u=4, a=2)
            Bv = Bk[0][ri].rearrange("p (f q) c -> p f q c", q=4)
            for q in range(4):
                for a in range(2):
                    sl = slice(64 * a, 64 * a + 64)
                    self.dve(lambda e, Zv=Zv, Bv=Bv, q=q, a=a, sl=sl: e.tensor_copy(
                        out=Zv[sl, :, q, q, a, :], in_=Bv[sl, :, q, :]), [bk], [zk])
        Zraw = [scr(f"Zraw{i}", [128, 4, 128], 1, BF16) for i in range(2)]
        for i in range(2):
            self.dve(lambda e, i=i: e.memset(Zraw[i], 0.0), [], [f"Zraw{i}"])
        nz = 0
        for k in range(8):
            cur = Bk[k % 2]
            ck = [f"Bk{k % 2}r", f"Bk{k % 2}i"]
            if k > 0:
                prv = Bk[(k - 1) % 2]
                pk = [f"Bk{(k - 1) % 2}r", f"Bk{(k - 1) % 2}i"]
                self.cmul("dve", cur[0], cur[1], cb(abreS), cb(abimS), prv[0], prv[1], tB1, tB2,
                          ["S_abre", "S_abim"] + pk, ck, "tB")
            for ri in range(2):
                Z = Zraw[nz % 2]
                zk = f"Zraw{nz % 2}"
                nz += 1
                Zv = Z.rearrange("p f (q a c) -> p f q a c", q=4, a=2)
                Bv = cur[ri].rearrange("p (f q) c -> p f q c", q=4)
                for a in range(2):
                    sl = slice(64 * a, 64 * a + 64)
                    self.dve(lambda e, Zv=Zv, Bv=Bv, a=a, sl=sl: e.tensor_copy(out=Zv[sl, :, :, a, :], in_=Bv[sl]),
                             [ck[ri]], [zk])
                b = self.bank()
                pbv = self.ps[b].bitcast(BF16)
                def tr(e, pbv=pbv, Z=Z):
                    for ft in range(4):
                        ins = e.transpose(out=pbv[:, ft * 128:(ft + 1) * 128], in_=Z[:, ft, :], identity=self.ident_b)
                    return ins
                self.pe(tr, [zk, "ident_b"], self.psk[b])
                self.act(lambda e, pbv=pbv, k=k, ri=ri: e.activation(out=self.XW[:, :, 7 - k, ri, :],
                                                                     in_=pbv[:, 0:512].rearrange("p (f c) -> p f c", f=4),
                                                                     func=AF.Copy), self.psk[b], ["XW"])

    def setup_s5_p2(self):
        L = self.load
        scr = lambda n, s, dt=F32: self.scr(n, s, 2, dt)
        PS_re, PS_im, ZB = self.PS_re, self.PS_im, self.ZB
        self.bpool = "B"
        CS = [scr("CSr", [128, 16, 32]), scr("CSi", [128, 16, 32])]
        CSb = [scr("CSbr", [128, 16, 32], BF16), scr("CSbn", [128, 16, 32], BF16)]
        CT, ZC = scr("CT", [128, 4, 64]), scr("ZC", [128, 4, 128])
        for ri, src in enumerate((self.c_re, self.c_im)):
            ck = "CSr" if ri == 0 else "CSi"
            L(CT, bass.AP(src, 0, [[64, 128], [8192, 4], [1, 64]]), writes=["CT"])
            self.dve(lambda e: e.tensor_copy(out=ZC.rearrange("p f (a c) -> p f a c", a=2),
                                             in_=bc(CT.unsqueeze(2), [128, 4, 2, 64])), ["CT"], ["ZC"])
            b = self.bank()
            def trc(e, b=b):
                for ft in range(4):
                    ins = e.transpose(out=self.ps[b][:, ft * 128:(ft + 1) * 128], in_=ZC[:, ft, :], identity=self.ident_f)
                return ins
            self.pe(trc, ["ZC", "ident_f"], self.psk[b])
            self.dve(lambda e, b=b, ri=ri: e.tensor_tensor(
                out=CS[ri].rearrange("p s (a h) -> p s a h", a=2),
                in0=self.ps[b].rearrange("p (s a h) -> p s a h", a=2, h=16),
                in1=bc(self.maskS.unsqueeze(1).unsqueeze(3), [128, 16, 2, 16]), op=ALU.mult),
                self.psk[b] + ["maskS"], [ck])
        self.dve(lambda e: e.tensor_copy(out=CSb[0], in_=CS[0]), ["CSr"], ["CSbr"])
        self.dve(lambda e: e.tensor_scalar(out=CSb[1], in0=CS[1], scalar1=-1.0, scalar2=None, op0=ALU.mult), ["CSi"], ["CSbn"])
        yield
        tC1, tC2, cimt, clr = scr("tC1", [128, 16, 32]), scr("tC2", [128, 16, 32]), scr("cimt", [128, 16, 32]), scr("clr", [128, 16, 32])
        TT = lambda o, a, b, op: (lambda e: e.tensor_tensor(out=o, in0=a, in1=b, op=op))
        for k in range(1, 9):
            pr = bc(PS_re[:, k, :].unsqueeze(2), [128, 16, 32])
            pi_ = bc(PS_im[:, k, :].unsqueeze(2), [128, 16, 32])
            rk = ["CSr", "CSi", f"PSre{k}", f"PSim{k}"]
            self.dve(TT(tC1, CS[0], pr, ALU.mult), rk, ["tC1"])
            self.dve(TT(tC2, CS[1], pi_, ALU.mult), rk, ["tC2"])
            self.dve(TT(self.CW[:, :, k - 1, 0, :], tC1, tC2, ALU.subtract), ["tC1", "tC2"], [f"CW{k - 1}"])
            self.dve(TT(tC1, CS[0], pi_, ALU.mult), rk, ["tC1"])
            self.dve(TT(tC2, CS[1], pr, ALU.mult), rk, ["tC2"])
            self.dve(lambda e, k=k: e.scalar_tensor_tensor(out=self.CW[:, :, k - 1, 1, :], in0=tC1, scalar=-1.0, in1=tC2,
                                                           op0=ALU.mult, op1=ALU.subtract), ["tC1", "tC2"], [f"CW{k - 1}"])
        yield
        self.bpool = "B"
        kb = [self.bank(), self.bank()]
        for k in range(8):
            b, co = kb[k // 4], (k % 4) * 128
            def kmm(e, b=b, k=k, co=co):
                for ft in range(4):
                    n = 0
                    for q in range(4):
                        st = 4 * ft + q
                        for ri in range(2):
                            rhs = CSb[ri][:, st, :] if k == 0 else self.CW[:, st, k - 1, ri, :]
                            ins = e.matmul(self.ps[b][:, co + ft * 32:co + (ft + 1) * 32], lhsT=ZB[ri][:, st, :],
                                           rhs=rhs, start=(n == 0), stop=(n == 7))
                            n += 1
                return ins
            self.pe(kmm, ["ZBr", "ZBi"] + (["CSbr", "CSbn"] if k == 0 else [f"CW{k - 1}"]), self.psk[b])
        for k in range(8):
            b, co = kb[k // 4], (k % 4) * 128
            self.dve(lambda e, k=k, b=b, co=co: e.tensor_tensor(
                out=self.KW[:, :, k, :].rearrange("p f (u c) -> p f u c", u=4),
                in0=bc(self.ps[b][:, co:co + 128].rearrange("p (f c) -> p f c", f=4).unsqueeze(2), [128, 4, 4, 32]),
                in1=bc(self.maskQ.unsqueeze(1).unsqueeze(3), [128, 4, 4, 32]), op=ALU.mult),
                self.psk[b] + ["maskQ"], ["KW"])
        self.S.alias(["CW"], [f"CW{i}" for i in range(8)])
        self.act(lambda e: e.activation(out=self.CW[:, 0, 0, 0, 0:1], in_=self.CW[:, 0, 0, 0, 0:1], func=AF.Copy), ["CW0"], ["CW"])
        yield

    def rstd(self, ssv, out, n, rk, wk, st):
        v, y, t = self.rtmp[st]
        npart = ssv.shape[0]
        v, y, t = v[0:npart, 0:n], y[0:npart, 0:n], t[0:npart, 0:n]
        kv, ky, kt = f"rt_v{st}", f"rt_y{st}", f"rt_t{st}"
        Dv = self.dve
        Dv(lambda e: e.tensor_scalar(out=v, in0=ssv, scalar1=1.0 / D, scalar2=EPS, op0=ALU.mult, op1=ALU.add),
           rk, [kv])
        vi, yi = v.bitcast(I32), y.bitcast(I32)
        Dv(lambda e: e.tensor_single_scalar(out=yi, in_=vi, scalar=1, op=ALU.arith_shift_right), [kv], [ky])
        Dv(lambda e: e.tensor_scalar(out=yi, in0=yi, scalar1=-1.0, scalar2=1597463007.0, op0=ALU.mult, op1=ALU.add),
           [ky], [ky])
        for it in range(3):
            Dv(lambda e: e.tensor_tensor(out=t, in0=y, in1=y, op=ALU.mult), [ky], [kt])
            Dv(lambda e: e.tensor_tensor(out=t, in0=t, in1=v, op=ALU.mult), [kt, kv], [kt])
            Dv(lambda e: e.tensor_scalar(out=t, in0=t, scalar1=-0.5, scalar2=1.5, op0=ALU.mult, op1=ALU.add),
               [kt], [kt])
            o = out if it == 2 else y
            Dv(lambda e, o=o: e.tensor_tensor(out=o, in0=y, in1=t, op=ALU.mult), [ky, kt],
               wk if it == 2 else [ky])

    def sumsq(self, xt, xk, NB, npart, st):
        for nb in range(NB):
            nx = len(self.xb[0])
            xb = self.xb[0][nb % nx]
            self.act(lambda e, nb=nb, xb=xb: e.activation(out=xb[0:npart], in_=xt[0:npart, nb, :], func=AF.Square,
                                                         accum_out=self.ss[st][0:npart, nb:nb + 1]),
                     [xk[nb]], [f"xb0{nb % nx}", f"ss{st}_{nb}"])
        self.rstd(self.ss[st][0:npart, 0:NB], self.rs[st][0:npart, 0:NB], NB, [f"ss{st}_{nb}" for nb in range(NB)],
                  [f"rs{st}"], st)

    def norm_T(self, xt, xk, T, gam, gk, dstT, dk, st, do_sumsq=True):
        NB = (T + 127) // 128
        npart = min(T, 128)
        if do_sumsq:
            self.sumsq(xt, xk, NB, npart, st)
        for nb in range(NB):
            nx = len(self.xb[0])
            xb = self.xb[0][nb % nx]
            bk = f"xb0{nb % nx}"
            self.act(lambda e, nb=nb, xb=xb: e.activation(out=xb[0:npart], in_=xt[0:npart, nb, :], func=AF.Copy,
                                                         scale=self.rs[st][0:npart, nb:nb + 1]),
                     [xk[nb], f"rs{st}"], [bk])
            for half in range(2):
                pb = self.ps[6 + half].bitcast(BF16)
                hk = self.psk[6 + half][0]
                def tr(e, nb=nb, xb=xb, half=half, pb=pb):
                    for k4 in range(4):
                        kc = half * 4 + k4
                        ins = e.transpose(out=pb[:, k4 * 128: k4 * 128 + npart],
                                          in_=xb[0:npart, kc * 128:(kc + 1) * 128], identity=self.ident_b[0:npart, 0:npart])
                    return ins
                self.pe(tr, [bk, "ident_b"], [hk])
                src = pb[:, 0:512].rearrange("p (k t) -> p k t", k=4)[:, :, 0:npart]
                dst = dstT[:, half * 4:(half + 1) * 4, nb * 128: nb * 128 + npart]
                g = bc(gam[:, half * 4:(half + 1) * 4].unsqueeze(2), [128, 4, npart])
                self.dve(lambda e, src=src, dst=dst, g=g: e.tensor_tensor(out=dst, in0=src, in1=g, op=ALU.mult),
                         [hk, gk], [f"{dk}{half}_{nb}"])
        return [f"{dk}{h}_{nb}" for h in range(2) for nb in range(NB)]

    def mkctx(self, k, kind, t0, T, last):
        c = type("Ctx", (), {})()
        c.k, c.kind, c.t0, c.T, c.last = k, kind, t0, T, last
        c.NB = (T + 127) // 128
        c.npart = min(T, 128)
        c.xt = self.xt[k % 2]
        c.xk = [f"xt{k % 2}_{nb}" for nb in range(c.NB)]
        c.Lc = LP if kind == "p" else LS
        c.NC = T // c.Lc
        c.toff = LMAX - c.Lc
        A = self.arB
        v4 = lambda lo: A[:, lo:lo + 2048].rearrange("p (g t) -> p g t", g=4)[:, :, 0:T]
        c.diffb = v4(0)
        c.Sbf = A[:, 0:2048].rearrange("p (r s c) -> p r s c", r=2, s=16)[:, :, :, 0:c.NC]
        c.uT = v4(2048)
        c.aout = v4(4096)
        c.zbf = v4(6144)
        c.bo = v4(8192)
        c.mbf = A[:, 10240:14336].rearrange("p (g t) -> p g t", g=8)[:, :, 0:T]
        c.fbf = A[:, 6144:6144 + NFC * 512].rearrange("p (g t) -> p g t", g=NFC)[:, :, 0:T]
        c.b1keys = [f"zbf{g}" for g in range(4)] + [f"bo{g}" for g in range(4)] + [f"mbf{j}" for j in range(8)]
        c.fkeys = [f"f{i}" for i in range(NFC)]
        if kind == "p":
            c.colv = lambda ap2, i: ap2.rearrange("p (c l) -> p c l", l=c.Lc)[:, :, i]
            c.colrange = None
        else:
            c.colv = lambda ap2, i: ap2[:, i * NS:(i + 1) * NS]
            c.colrange = lambda ap2, lo, hi: ap2[:, lo * NS:hi * NS]
        return c

    def load_x(self, c):
        if getattr(c, "xloaded", False):
            return
        c.xloaded = True
        src = self.xp.ap() if c.kind == "p" else self.xs.ap()
        for nb in range(c.NB):
            self.load(c.xt[0:c.npart, nb, :], src[c.t0 + nb * 128: c.t0 + nb * 128 + c.npart, :], writes=[c.xk[nb]])

    def stageA(self, c):
        S = self.S
        k, kind, t0, T, last, NB, npart = c.k, c.kind, c.t0, c.T, c.last, c.NB, c.npart
        xt, xk, Lc, NC, toff = c.xt, c.xk, c.Lc, c.NC, c.toff
        diffb, Sbf, uT, aout, colv = c.diffb, c.Sbf, c.uT, c.aout, c.colv
        self.bpool = "A"
        self.load_x(c)
        xnT = self.xnT
        self.sumsq(xt, xk, NB, npart, 0)
        yield
        c.xnk = xnk = self.norm_T(xt, xk, T, self.gm, "gm", xnT, "xnT", 0, do_sumsq=False)
        yield
        self.bpool = "A"
        S.alias(["E"], ["S2"])
        S.alias([f"diff{g}" for g in range(4)], ["Sbf"])
        if kind == "p":
            E = self.fA[:, 0:4 * (HIST + T)].rearrange("p (g t) -> p g t", g=4)
            Enew = E[:, :, HIST:HIST + T]
            self.pool(lambda e: e.tensor_copy(out=E[:, :, 0:HIST], in_=self.stash), ["stash"], ["E"])
        else:
            E = self.fA[:, 0:4 * NS * 19].rearrange("p (g s t) -> p g s t", g=4, s=NS)
            self.sample_hist(E)
        ring, rk = self.next_w("win0")
        W = ring.rearrange("p (k n) -> p k n", k=8)
        for g in range(4):
            b = self.bank()
            def mm(e, g=g, b=b, W=W):
                for kc in range(8):
                    ins = e.matmul(self.ps[b][:, 0:T], lhsT=W[:, kc, g * 128:(g + 1) * 128], rhs=xnT[:, kc, 0:T],
                                   start=(kc == 0), stop=(kc == 7))
                return ins
            self.pe(mm, rk + xnk, self.psk[b])
            if kind == "p":
                self.act(lambda e, g=g, b=b: e.activation(out=Enew[:, g, :], in_=self.ps[b][:, 0:T], func=AF.Copy),
                         self.psk[b], ["E"])
            else:
                self.act(lambda e, g=g, b=b: e.activation(
                    out=E[:, g, :, HIST:HIST + TS].rearrange("p s t -> p t s"),
                    in_=self.ps[b][:, 0:T].rearrange("p (t s) -> p t s", s=NS), func=AF.Copy),
                    self.psk[b], ["E"])
        if kind == "s" or last:
            b = self.bank()
            nbl = NB - 1
            def mmt(e, b=b, W=W, nbl=nbl):
                for kc in range(8):
                    ins = e.matmul(self.ps[b][0:npart, :], lhsT=xnT[:, kc, nbl * 128: nbl * 128 + npart], rhs=W[:, kc, :],
                                   start=(kc == 0), stop=(kc == 7))
                return ins
            self.pe(mmt, rk + xnk, self.psk[b])
            ut = self.fC[0:npart, 2, :]
            self.dve(lambda e, b=b: e.tensor_copy(out=ut, in_=self.ps[b][0:npart, :]), self.psk[b], ["fC2"])
            if kind == "p":
                self.store(self.o_pool_p.ap(), ut[128 - HIST:128, :], reads=["fC2"])
            else:
                for t in range(TS):
                    self.store(self.o_pool_s.ap()[:, HIST - TS + t, :], ut[t * NS:(t + 1) * NS, :], reads=["fC2"])
        if kind == "p":
            self.pool(lambda e: e.tensor_copy(out=self.stash, in_=E[:, :, T:T + HIST]), ["E"], ["stash"])
        yield
        self.bpool = "A"
        if kind == "p":
            Ws = [self.fB[:, i * 528: i * 528 + HIST + T] for i in range(2)]
        for g in range(4):
            nlev = g + 1
            w = 2 ** nlev
            if kind == "p":
                ln = HIST + T
                cur = E[:, g, :]
                for lev in range(nlev):
                    sh = 2 ** lev
                    dstb = Ws[lev % 2]
                    self.pool(lambda e, cur=cur, dstb=dstb, sh=sh, ln=ln: e.tensor_tensor(
                        out=dstb[:, sh:ln], in0=cur[:, sh:ln], in1=cur[:, 0:ln - sh], op=ALU.add),
                        ["E", "PT"], ["PT"])
                    cur = dstb
                wsum = cur[:, HIST:HIST + T]
                sc = Ws[nlev % 2][:, 0:T]
                self.pool(lambda e, wsum=wsum, sc=sc, w=w: e.tensor_scalar(out=sc, in0=wsum, scalar1=1.0 / w, scalar2=None,
                                                                          op0=ALU.mult), ["PT"], ["PT"])
                if t0 == 0:
                    self.pool(lambda e, wsum=wsum, sc=sc, w=w: e.tensor_tensor(out=sc[:, 0:w - 1], in0=wsum[:, 0:w - 1],
                                                                              in1=self.invc[:, 0:w - 1], op=ALU.mult),
                              ["PT", "invc"], ["PT"])
                self.pool(lambda e, sc=sc, g=g: e.tensor_tensor(out=diffb[:, g, :], in0=sc, in1=Enew[:, g, :], op=ALU.subtract),
                          ["PT", "E"], [f"diff{g}"])
            else:
                ln = 19
                bufs = [self.fB[:, i * 320: i * 320 + NS * ln].rearrange("p (s t) -> p s t", s=NS) for i in range(2)]
                cur = E[:, g]
                for lev in range(nlev):
                    sh = 2 ** lev
                    dstb = bufs[lev % 2]
                    self.pool(lambda e, cur=cur, dstb=dstb, sh=sh: e.tensor_tensor(
                        out=dstb[:, :, sh:ln], in0=cur[:, :, sh:ln], in1=cur[:, :, 0:ln - sh], op=ALU.add),
                        ["E", "PT"], ["PT"])
                    cur = dstb
                wsum = cur[:, :, HIST:ln]
                sc = bufs[nlev % 2][:, :, 0:TS]
                self.pool(lambda e, wsum=wsum, sc=sc, w=w: e.tensor_scalar(out=sc, in0=wsum, scalar1=1.0 / w, scalar2=None,
                                                                          op0=ALU.mult), ["PT"], ["PT"])
                self.pool(lambda e, sc=sc, g=g: e.tensor_tensor(
                    out=diffb[:, g, :].rearrange("p (t s) -> p s t", s=NS), in0=sc, in1=E[:, g, :, HIST:ln], op=ALU.subtract),
                    ["PT", "E"], [f"diff{g}"])
        ring, rk = self.next_w("win1")
        W = ring.rearrange("p (k n) -> p k n", k=8)
        for ft in range(4):
            b = self.bank()
            def mm(e, ft=ft, b=b, W=W):
                for kc in range(8):
                    ins = e.matmul(self.ps[b][:, 0:T], lhsT=W[:, kc, ft * 128:(ft + 1) * 128], rhs=xnT[:, kc, 0:T],
                                   start=(kc == 0), stop=(kc == 7))
                return ins
            self.pe(mm, rk + xnk, self.psk[b])
            self.act(lambda e, ft=ft, b=b: e.activation(out=uT[:, ft, :], in_=self.ps[b][:, 0:T], func=AF.Copy),
                     self.psk[b], [f"uT{ft}"])
        yield
        self.bpool = "A"
        X2 = self.fX[:, 0:NC * 32].rearrange("p (c r s) -> p c r s", r=2, s=16)
        per_bank = 512 // NC if NC * 8 <= 512 else 8
        per_bank = min(per_bank, 8)
        for rnd in range(2):
            xb2 = [self.bank(), self.bank()]
            def xmm(e, rnd=rnd, xb2=xb2):
                for s8 in range(8):
                    st = rnd * 8 + s8
                    ft, q = st // 4, st % 4
                    for ri in range(2):
                        out = self.ps[xb2[ri]][:, s8 * NC:(s8 + 1) * NC]
                        for i in range(Lc):
                            ins = e.matmul(out, lhsT=self.XW[32 * q:32 * q + 32, ft, i + toff, ri, :],
                                           rhs=colv(uT[32 * q:32 * q + 32, ft, :], i),
                                           start=(i == 0), stop=(i == Lc - 1), tile_position=(32 * q, 0))
                return ins
            self.pe(xmm, ["XW"] + [f"uT{ft}" for ft in range(4)], self.psk[xb2[0]] + self.psk[xb2[1]])
            for ri in range(2):
                self.act(lambda e, rnd=rnd, ri=ri, xb2=xb2: e.activation(
                    out=X2[:, :, ri, rnd * 8:(rnd + 1) * 8].rearrange("p c s -> p s c"),
                    in_=self.ps[xb2[ri]][:, 0:8 * NC].rearrange("p (s c) -> p s c", c=NC), func=AF.Copy),
                    self.psk[xb2[ri]], ["X2"])
        yield
        self.bpool = "A"
        for g in range(4):
            b = self.bank()
            self.pe(lambda e, g=g, b=b: e.matmul(self.ps[b][:, 0:T], lhsT=self.pw[:, g, :], rhs=diffb[:, g, :],
                                                 start=True, stop=True), ["pw", f"diff{g}"], self.psk[b])
            self.act(lambda e, g=g, b=b: e.activation(out=aout[:, g, :], in_=self.ps[b][:, 0:T], func=AF.Copy,
                                                      scale=self.pscale[:, g:g + 1]), self.psk[b] + ["pscale"], [f"aout{g}"])
        yield
        S.alias(["S2"], ["E"])
        Mrr, Mii = self.Mrr[Lc], self.Mii[Lc]
        if kind == "p":
            S2 = self.fA[:, 0:(NC + 1) * 32].rearrange("p (c r s) -> p c r s", r=2, s=16)
            SC = self.dve if k == 0 else self.pool
            SC(lambda e: e.tensor_copy(out=S2[:, 0], in_=self.carry), ["carry"], ["S2"])
            for cc in range(NC):
                t1, t2 = self.sct[(2 * cc) % 4], self.sct[(2 * cc + 1) % 4]
                k1, k2 = f"sct{(2 * cc) % 4}", f"sct{(2 * cc + 1) % 4}"
                prev = S2[:, cc]
                prevsw = bass.AP(prev.tensor, prev.offset + 16, [list(prev.ap[0]), [-16, 2], [1, 16]])
                SC(lambda e, t1=t1, prev=prev: e.tensor_tensor(out=t1, in0=prev, in1=Mrr, op=ALU.mult),
                         ["S2", f"M{Lc}"], [k1])
                SC(lambda e, t2=t2, prevsw=prevsw: e.tensor_tensor(out=t2, in0=prevsw, in1=Mii, op=ALU.mult),
                         ["S2", f"M{Lc}"], [k2])
                SC(lambda e, t1=t1, t2=t2: e.tensor_tensor(out=t1, in0=t1, in1=t2, op=ALU.add), [k1, k2], [k1])
                SC(lambda e, t1=t1, cc=cc: e.tensor_tensor(out=S2[:, cc + 1], in0=t1, in1=X2[:, cc], op=ALU.add),
                         [k1, "X2"], ["S2"])
                if cc % 4 == 3:
                    yield
                    self.bpool = "A"
            SC(lambda e: e.tensor_copy(out=self.carry, in_=S2[:, NC]), ["S2"], ["carry"])
            Sprev = S2[:, 0:NC]
            Sfinal = None
        else:
            S2 = self.fA[:, 0:2 * NC * 32].rearrange("p (u c r s) -> p u c r s", u=2, r=2, s=16)
            self.sample_state_in(S2[:, 0])
            tA = self.fB[:, 0:NC * 32].rearrange("p (c r s) -> p c r s", r=2, s=16)
            tB = self.fB[:, 512:512 + NC * 32].rearrange("p (c r s) -> p c r s", r=2, s=16)
            S0 = S2[:, 0]
            S0sw = bass.AP(S0.tensor, S0.offset + 16, [list(S0.ap[0]), [32, NC], [-16, 2], [1, 16]])
            mrrb = bc(Mrr.unsqueeze(1), [128, NC, 2, 16])
            miib = bc(Mii.unsqueeze(1), [128, NC, 2, 16])
            self.dve(lambda e: e.tensor_tensor(out=tA, in0=S0, in1=mrrb, op=ALU.mult), ["S2", f"M{Lc}"], ["PT"])
            self.dve(lambda e: e.tensor_tensor(out=tB, in0=S0sw, in1=miib, op=ALU.mult), ["S2", f"M{Lc}", "PT"], ["PT"])
            self.dve(lambda e: e.tensor_tensor(out=tA, in0=tA, in1=tB, op=ALU.add), ["PT"], ["PT"])
            self.dve(lambda e: e.tensor_tensor(out=S2[:, 1], in0=tA, in1=X2, op=ALU.add), ["PT", "X2"], ["S2"])
            Sprev = S2[:, 0]
            Sfinal = S2[:, 1]
        S.alias(["Sbf"], [f"diff{g}" for g in range(4)])
        for ri in range(2):
            self.act(lambda e, ri=ri: e.activation(out=Sbf[:, ri], in_=Sprev[:, :, ri, :].rearrange("p c s -> p s c"),
                                                   func=AF.Copy), ["S2"], ["Sbf"])
        if kind == "s":
            self.sample_state_out(Sfinal)
        elif last:
            self.prompt_state_out()
        yield

    def stageB1(self, c):
        S = self.S
        k, kind, t0, T, last, NB, npart = c.k, c.kind, c.t0, c.T, c.last, c.NB, c.npart
        xt, xk, Lc, NC = c.xt, c.xk, c.Lc, c.NC
        Sbf, uT, aout, zbf, bo, mbf, colv, colrange = c.Sbf, c.uT, c.aout, c.zbf, c.bo, c.mbf, c.colv, c.colrange
        xnT, xnk = self.xnT, c.xnk
        self.bpool = "all"
        if kind == "s" and self.sample_first_idx is None:
            self.sample_first_idx = self.stream_pos
        S.alias(c.b1keys, c.fkeys)
        for ft in range(4):
            b = self.bank()
            psY = self.ps[b][:, 0:T]
            def ymm(e, ft=ft, psY=psY):
                ins = e.matmul(psY, lhsT=self.KW[:, ft, 0, :], rhs=uT[:, ft, :], start=True, stop=False)
                for j in range(1, Lc):
                    if kind == "p":
                        for i in range(j, Lc):
                            ins = e.matmul(colv(psY, i), lhsT=self.KW[:, ft, j, :], rhs=colv(uT[:, ft, :], i - j),
                                           start=False, stop=False)
                    else:
                        ins = e.matmul(colrange(psY, j, Lc), lhsT=self.KW[:, ft, j, :], rhs=colrange(uT[:, ft, :], 0, Lc - j),
                                       start=False, stop=False)
                for ri in range(2):
                    for i in range(Lc):
                        for q in range(4):
                            st = 4 * ft + q
                            ins = e.matmul(colv(psY[32 * q:32 * q + 32, :], i), lhsT=self.CW[:, st, i, ri, :],
                                           rhs=Sbf[:, ri, st, :], start=False, stop=(ri == 1 and i == Lc - 1),
                                           tile_position=(0, 32 * q))
                return ins
            self.pe(ymm, ["KW", "CW", "Sbf", f"uT{ft}"], self.psk[b])
            yf = self.fC[:, ft % 2, 0:T]
            self.dve(lambda e, ft=ft, psY=psY, yf=yf: e.scalar_tensor_tensor(out=yf, in0=uT[:, ft, :], scalar=self.dsk[:, ft:ft + 1],
                                                                            in1=psY, op0=ALU.mult, op1=ALU.add),
                     self.psk[b] + [f"uT{ft}", "dsk"], [f"fC{ft % 2}"])
            self.act(lambda e, ft=ft, yf=yf: e.activation(out=zbf[:, ft, :], in_=yf, func=AF.Gelu_apprx_tanh),
                     [f"fC{ft % 2}"], [f"zbf{ft}"])
        for fo in range(4):
            b = self.bank()
            def gmm(e, fo=fo, b=b):
                for kc in range(4):
                    ins = e.matmul(self.ps[b][:, 0:T], lhsT=self.glu[:, kc, fo * 128:(fo + 1) * 128], rhs=zbf[:, kc, :],
                                   start=(kc == 0), stop=(kc == 3))
                return ins
            self.pe(gmm, ["glu"] + [f"zbf{ft}" for ft in range(4)], self.psk[b])
            thl = self.fC[:, 2, 0:T] if fo % 2 == 0 else self.fC[:, 1, 0:T]
            tk_ = "fC2" if fo % 2 == 0 else "fC1"
            self.act(lambda e, fo=fo, b=b, thl=thl: e.activation(out=thl, in_=self.ps[b][:, 0:T], func=AF.Tanh, scale=0.5,
                                                                 bias=self.hgb[:, fo:fo + 1]), self.psk[b] + ["hgb"], [tk_])
            self.dve(lambda e, fo=fo, thl=thl: e.scalar_tensor_tensor(out=bo[:, fo, :], in0=thl, scalar=1.0, in1=zbf[:, fo, :],
                                                                      op0=ALU.add, op1=ALU.mult), [tk_, f"zbf{fo}"], [f"bo{fo}"])
        Wbp, rkp = self.wbp_sb, ["wbp"]
        Wbs, rkb = self.wbs_sb, ["wbs"]
        th0 = self.fC[:, 0, 0:T]
        th1 = self.fC[:, 1, 0:T]
        m0 = self.fC[:, 2, 0:T]
        for jp in range(4):
            ring, rk = self.next_w(f"win{2 + jp}")
            W = ring.rearrange("p (k n) -> p k n", k=8)
            for jj in range(2):
                j = 2 * jp + jj
                for br in range(2):
                    col = (br * 2 + jj) * 128
                    b = self.bank()
                    def mm(e, col=col, b=b, W=W):
                        for kc in range(8):
                            ins = e.matmul(self.ps[b][:, 0:T], lhsT=W[:, kc, col:col + 128], rhs=xnT[:, kc, 0:T],
                                           start=(kc == 0), stop=(kc == 7))
                        return ins
                    self.pe(mm, rk + xnk, self.psk[b])
                    tht = th0 if br == 0 else th1
                    tk = "fC0" if br == 0 else "fC1"
                    self.act(lambda e, b=b, tht=tht: e.activation(out=tht, in_=self.ps[b][:, 0:T], func=AF.Tanh, scale=0.5),
                             self.psk[b], [tk])
                    b2 = self.bank()
                    Wb = Wbp if br == 0 else Wbs
                    src4 = aout if br == 0 else bo
                    def bmm(e, b2=b2, Wb=Wb, src4=src4, j=j):
                        for kc in range(4):
                            ins = e.matmul(self.ps[b2][:, 0:T], lhsT=Wb[:, kc, j * 128:(j + 1) * 128], rhs=src4[:, kc, :],
                                           start=(kc == 0), stop=(kc == 3))
                        return ins
                    self.pe(bmm, (rkp if br == 0 else rkb) + [f"{'aout' if br == 0 else 'bo'}{g}" for g in range(4)],
                            self.psk[b2])
                    if br == 0:
                        self.dve(lambda e, b2=b2: e.scalar_tensor_tensor(out=m0, in0=th0, scalar=1.0, in1=self.ps[b2][:, 0:T],
                                                                         op0=ALU.add, op1=ALU.mult),
                                 ["fC0"] + self.psk[b2], ["fC2"])
                    else:
                        self.dve(lambda e, b2=b2: e.scalar_tensor_tensor(out=th1, in0=th1, scalar=1.0, in1=self.ps[b2][:, 0:T],
                                                                         op0=ALU.add, op1=ALU.mult),
                                 ["fC1"] + self.psk[b2], ["fC1"])
                        self.dve(lambda e, j=j: e.scalar_tensor_tensor(out=mbf[:, j, :], in0=th1, scalar=0.5, in1=m0,
                                                                       op0=ALU.mult, op1=ALU.add),
                                 ["fC1", "fC2"], [f"mbf{j}"])
        yield
        self.bpool = "all"
        Wo = []
        for half in range(2):
            ring, rk = self.next_w(f"wout{half}", nopref=True)
            Wo.append((ring.rearrange("p (k n) -> p k n", k=8), rk))
        for nb in range(NB):
            for half in range(2):
                W, rk = Wo[half]
                b = self.bank()
                def omm(e, nb=nb, b=b, W=W):
                    for kc in range(8):
                        ins = e.matmul(self.ps[b][0:npart, :], lhsT=mbf[:, kc, nb * 128: nb * 128 + npart], rhs=W[:, kc, :],
                                       start=(kc == 0), stop=(kc == 7))
                    return ins
                self.pe(omm, rk + [f"mbf{j}" for j in range(8)], self.psk[b])
                hv = xt[0:npart, nb, half * 512:(half + 1) * 512]
                self.dve(lambda e, b=b, hv=hv: e.scalar_tensor_tensor(out=hv, in0=self.ps[b][0:npart, :], scalar=0.5, in1=hv,
                                                                      op0=ALU.mult, op1=ALU.add),
                         self.psk[b] + [xk[nb]], [xk[nb]])
        if self.stream_order is not None:
            depth = 5 if (self.i_last0 is not None and self.stream_pos - 1 >= self.i_last0) else self.NRING
            self.prefetch(self.stream_pos - 1 + depth)
        yield
        self.sumsq(xt, xk, NB, npart, 1)
        yield
        c.hnk = self.norm_T(xt, xk, T, self.gn, "gn", self.hnT, "hnT", 1, do_sumsq=False)
        yield

    def stageF(self, c):
        S = self.S
        k, kind, t0, T, last, NB, npart = c.k, c.kind, c.t0, c.T, c.last, c.NB, c.npart
        xt, xk, fbf, hnk = c.xt, c.xk, c.fbf, c.hnk
        hnT = self.hnT
        S.alias(c.fkeys, c.b1keys)
        sl0 = self.fC[:, 0, 0:T]
        sl1 = self.fC[:, 1, 0:T]
        for j in range(11):
            self.bpool = "B" if j < 4 else "all"
            ring, rk = self.next_w(f"gu{j}")
            Wg = ring[:, 0:2048].rearrange("p (k n) -> p k n", k=8)
            Wu = ring[:, 2048:4096].rearrange("p (k n) -> p k n", k=8)
            for o in range(2):
                fc = 2 * j + o
                bg, bu = self.bank(), self.bank()
                def gm_(e, o=o, bg=bg, Wg=Wg):
                    for kc in range(8):
                        ins = e.matmul(self.ps[bg][:, 0:T], lhsT=Wg[:, kc, o * 128:(o + 1) * 128], rhs=hnT[:, kc, 0:T],
                                       start=(kc == 0), stop=(kc == 7))
                    return ins
                def um_(e, o=o, bu=bu, Wu=Wu):
                    for kc in range(8):
                        ins = e.matmul(self.ps[bu][:, 0:T], lhsT=Wu[:, kc, o * 128:(o + 1) * 128], rhs=hnT[:, kc, 0:T],
                                       start=(kc == 0), stop=(kc == 7))
                    return ins
                self.pe(gm_, rk + hnk, self.psk[bg])
                self.pe(um_, rk + hnk, self.psk[bu])
                sl = sl0 if fc % 2 == 0 else sl1
                sk = "fC0" if fc % 2 == 0 else "fC1"
                self.act(lambda e, bg=bg, sl=sl: e.activation(out=sl, in_=self.ps[bg][:, 0:T], func=AF.Silu),
                         self.psk[bg], [sk])
                self.dve(lambda e, bu=bu, sl=sl, fc=fc: e.tensor_tensor(out=fbf[:, fc, :], in0=sl, in1=self.ps[bu][:, 0:T],
                                                                       op=ALU.mult), [sk] + self.psk[bu], [f"f{fc}"])
            yield
        for p0 in range(0, NB, 2):
            self.bpool = "B"
            nbs = list(range(p0, min(p0 + 2, NB)))
            accs = [(nb, half) for nb in nbs for half in range(2)]
            abank = {a: self.bank() for a in accs}
            used = sorted(set(abank.values()))
            for f0 in range(0, NFC, 4):
                nf = min(4, NFC - f0)
                ring, rk = self.next_w(f"wd{f0}")
                W = ring[:, 0:nf * 1024].rearrange("p (f n) -> p f n", f=nf)
                def dmm(e, f0=f0, nf=nf, W=W, accs=accs, abank=abank):
                    for fi in range(nf):
                        fc = f0 + fi
                        for (nb, half) in accs:
                            ins = e.matmul(self.ps[abank[(nb, half)]][0:npart, :], lhsT=fbf[:, fc, nb * 128: nb * 128 + npart],
                                           rhs=W[:, fi, half * 512:(half + 1) * 512], start=(fc == 0), stop=(fc == NFC - 1))
                    return ins
                self.pe(dmm, rk + [f"f{f0 + fi}" for fi in range(nf)], sum([self.psk[b] for b in used], []))
                yield
                self.bpool = "B"
            for (nb, half) in accs:
                b = abank[(nb, half)]
                hv = xt[0:npart, nb, half * 512:(half + 1) * 512]
                self.dve(lambda e, b=b, hv=hv: e.tensor_tensor(out=hv, in0=self.ps[b][0:npart, :], in1=hv, op=ALU.add),
                         self.psk[b] + [xk[nb]], [xk[nb]])
        self.sumsq(xt, xk, NB, npart, 1)
        dst = self.yp.ap() if kind == "p" else self.ys.ap()
        for nb in range(NB):
            self.dve(lambda e, nb=nb: e.scalar_tensor_tensor(out=xt[0:npart, nb, :], in0=xt[0:npart, nb, :],
                                                             scalar=self.rs[1][0:npart, nb:nb + 1], in1=self.gf[0:npart],
                                                             op0=ALU.mult, op1=ALU.mult), [xk[nb], "rs1", "gf"], [xk[nb]])
            self.store(dst[t0 + nb * 128: t0 + nb * 128 + npart, :], xt[0:npart, nb, :], reads=[xk[nb]])
        yield

    def sample_hist(self, E):
        hb = self.fX
        sp = self.spool.ap().rearrange("s h c -> (s h) c")
        for blk in range(2):
            self.load(hb[0:120, blk * 512:(blk + 1) * 512], sp[blk * 120:(blk + 1) * 120, :], writes=["X2"])
        for g in range(4):
            b = self.bank()
            def tr(e, g=g, b=b):
                for blk in range(2):
                    ins = e.transpose(out=self.ps[b][:, blk * 120:(blk + 1) * 120],
                                      in_=hb[0:120, blk * 512 + g * 128: blk * 512 + (g + 1) * 128],
                                      identity=self.ident_f[0:120, 0:120])
                return ins
            self.pe(tr, ["X2", "ident_f"], self.psk[b])
            self.act(lambda e, g=g, b=b: e.activation(out=E[:, g, :, 0:HIST],
                                                      in_=self.ps[b][:, 0:240].rearrange("p (s h) -> p s h", h=HIST),
                                                      func=AF.Copy), self.psk[b], ["E"])
        t = self.S.op("pool", lambda e: e.dma_start(out=self.o_pool_s.ap()[:, 0:HIST - TS, :],
                                                    in_=self.spool.ap()[:, TS:HIST, :]), dma=True)
        self.final.append(t)

    def sample_state_in(self, S0):
        for ri, src in enumerate((self.sre, self.sim)):
            sin = self.sst[ri]
            self.load(sin, src.ap(), writes=self.sstk[ri])
            b = self.bank()
            def tr(e, sin=sin, b=b):
                for st in range(16):
                    ins = e.transpose(out=self.ps[b][:, st * NS:(st + 1) * NS], in_=sin[0:NS, st * 128:(st + 1) * 128],
                                      identity=self.ident_f[0:NS, 0:NS])
                return ins
            self.pe(tr, self.sstk[ri] + ["ident_f"], self.psk[b])
            self.act(lambda e, ri=ri, b=b: e.activation(out=S0[:, :, ri, :].rearrange("p c s -> p s c"),
                                                        in_=self.ps[b][:, 0:16 * NS].rearrange("p (s c) -> p s c", c=NS),
                                                        func=AF.Copy), self.psk[b], ["S2"])

    def sample_state_out(self, S1):
        for ri, dst in enumerate((self.o_re_s, self.o_im_s)):
            so = self.sst[ri]
            for h in range(4):
                b = self.bank()
                def tr(e, b=b, h=h, ri=ri):
                    for s4 in range(4):
                        st = 4 * h + s4
                        ins = e.transpose(out=self.ps[b][0:NS, s4 * 128:(s4 + 1) * 128], in_=S1[:, :, ri, st],
                                          identity=self.ident_f)
                    return ins
                self.pe(tr, ["S2", "ident_f"], self.psk[b])
                self.act(lambda e, b=b, h=h, so=so: e.activation(out=so[0:NS, h * 512:(h + 1) * 512], in_=self.ps[b][0:NS, :],
                                                                 func=AF.Copy), self.psk[b], self.sstk[ri])
            self.store(dst.ap(), so[0:NS, :], reads=self.sstk[ri])

    def prompt_state_out(self):
        b = self.bank()
        cf = self.carry.rearrange("p r s -> p (r s)")
        self.pe(lambda e, b=b: e.transpose(out=self.ps[b][0:32, 0:128], in_=cf, identity=self.ident_f), ["carry", "ident_f"],
                self.psk[b])
        so = self.msm.rearrange("p a b -> p (a b)")[0:32, 0:128]
        self.act(lambda e, b=b: e.activation(out=so, in_=self.ps[b][0:32, 0:128], func=AF.Copy), self.psk[b], ["msm"])
        self.store(self.o_re_p.ap(), so[0:16, :], reads=["msm"])
        self.store(self.o_im_p.ap(), so[16:32, :], reads=["msm"])

    def build(self, stage=None):
        import os
        stage = stage or os.environ.get("KSTAGE", "all")
        self.setup_consts()
        if stage == "consts":
            self.S.emit_all(self.final); return self.nc
        self.convert_weights()
        if stage == "convert":
            self.final += [self.S.last_w[k] for k in self.S.last_w if k.startswith("s_")]
            self.S.emit_all(self.final); return self.nc
        tiles = [("p", i * TP, TP) for i in range(SEQ // TP)] + [("s", 0, NS * TS)]
        if stage.startswith("tiles"):
            tiles = tiles[:int(stage[5:])]
        ctxs = [self.mkctx(k, kind, t0, T, last=(kind == "p" and t0 + T == SEQ)) for k, (kind, t0, T) in enumerate(tiles)]

        def run(g):
            for _ in g:
                pass

        self.setup_s5_p1()
        self.load(self.gm, self.norm_mix.ap().rearrange("(k p) -> p k", p=128), writes=["gm"], slow=True)
        self.prefetch(self.NRING)
        a0 = self.stageA(ctxs[0])
        next(a0)
        next(a0)
        k1 = (["E", "X2", "PT", "S2", "Sbf"] + [f"diff{g}" for g in range(4)] + [f"uT{g}" for g in range(4)]
              + [f"aout{g}" for g in range(4)])
        self.S.alias(k1, self.setup_keys[1])
        self.dve(lambda e: e.memset(self.fB, 0.0), [], ["PT"])
        self.load_small_vectors()
        p2 = self.setup_s5_p2()
        next(p2)
        self.load_resident()
        for _ in range(4):
            next(a0)
        next(p2)
        cvg = self.convert_weights_ffn()
        ca, aa = True, True
        while ca or aa:
            if ca:
                try:
                    next(cvg)
                except StopIteration:
                    ca = False
            if aa:
                try:
                    next(a0)
                except StopIteration:
                    aa = False
        run(p2)
        k2 = ([f"xt1_{nb}" for nb in range(4)] + ["fC0", "fC1", "fC2"] + [f"hnT{h}_{nb}" for h in range(2) for nb in range(4)]
              + [f"zbf{g}" for g in range(4)] + [f"bo{g}" for g in range(4)] + [f"mbf{j}" for j in range(8)]
              + [f"f{c}" for c in range(NFC)])
        self.S.alias(k2, self.setup_keys[2])
        print("sbuf bytes remaining:", self.nc.sbuf_bytes_remaining, "scratch", {p: (s["i"], s["off"]) for p, s in self.scr_state.items()})
        for k in range(len(ctxs)):
            a = self.stageA(ctxs[k + 1]) if k + 1 < len(ctxs) else None
            if a is not None:
                self.load_x(ctxs[k + 1])
            b1 = self.stageB1(ctxs[k])
            next(b1)
            if a is not None:
                next(a)
            next(b1)
            next(b1)
            if a is not None:
                next(a)
                next(a)
            run(b1)
            f = self.stageF(ctxs[k])
            fa, aa = True, a is not None
            while fa or aa:
                if fa:
                    try:
                        next(f)
                    except StopIteration:
                        fa = False
                if aa:
                    try:
                        next(a)
                    except StopIteration:
                        aa = False
        if self.stream_order is None:
            return self.rec, self.sample_first_idx
        self.S.emit_all(self.final)
        return self.nc


def build_program():
    rec, i_last0 = Builder().build()
    return Builder(stream_order=rec, i_last0=i_last0).build()


def perm_w_in(w):
    cols = [np.arange(0, 1024)]
    for jp in range(4):
        cols.append(np.arange(1024 + 256 * jp, 1024 + 256 * (jp + 1)))
        cols.append(np.arange(2048 + 256 * jp, 2048 + 256 * (jp + 1)))
    return np.ascontiguousarray(w[:, np.concatenate(cols)])


def make_in_maps(inp):
    f = lambda a: np.ascontiguousarray(np.asarray(a, dtype=np.float32))
    shared = {
        "norm_mix": f(inp["norm_mix"][0]), "w_in": perm_w_in(f(inp["w_in"][0])), "pool_w": f(inp["pool_w"][0]),
        "pool_scale": f(inp["pool_scale"][0]), "ssm_a_re": f(inp["ssm_a_re"][0]).reshape(-1),
        "ssm_a_im": f(inp["ssm_a_im"][0]).reshape(-1), "ssm_log_dt": f(inp["ssm_log_dt"][0]),
        "ssm_b_re": f(inp["ssm_b_re"][0]).reshape(-1), "ssm_b_im": f(inp["ssm_b_im"][0]).reshape(-1),
        "ssm_c_re": f(inp["ssm_c_re"][0]).reshape(-1), "ssm_c_im": f(inp["ssm_c_im"][0]).reshape(-1),
        "ssm_d": f(inp["ssm_d"][0]), "glu_w": f(inp["glu_w"][0]), "glu_b": f(inp["glu_b"][0]),
        "w_branch_pool": f(inp["w_branch_pool"][0]), "w_branch_ssm": f(inp["w_branch_ssm"][0]),
        "w_out": f(inp["w_out"][0]), "norm_ffn": f(inp["norm_ffn"][0]), "ffn_w_gate": f(inp["ffn_w_gate"][0]),
        "ffn_w_up": f(inp["ffn_w_up"][0]), "ffn_w_down": f(inp["ffn_w_down"][0]), "norm_final": f(inp["norm_final"]),
    }
    maps = []
    for b in range(N_CORES):
        m = dict(shared)
        m["xp"] = f(inp["x_prompt"][b])
        m["xs"] = f(np.transpose(np.asarray(inp["x_sample"])[NS * b:NS * (b + 1)], (1, 0, 2)).reshape(NS * TS, D))
        m["spool"] = f(inp["state_pool"][0, NS * b:NS * (b + 1)])
        m["sre"] = f(np.asarray(inp["state_ssm_re"])[0, NS * b:NS * (b + 1)].reshape(NS, 2048))
        m["sim"] = f(np.asarray(inp["state_ssm_im"])[0, NS * b:NS * (b + 1)].reshape(NS, 2048))
        maps.append(m)
    return maps


_CACHE = {}


def kernel(**inp):
    if "nc" not in _CACHE:
        _CACHE["nc"] = build_program()
    nc = _CACHE["nc"]
    maps = make_in_maps(inp)
    res = run_bass_kernel_spmd(nc, maps, core_ids=list(range(N_CORES)))
    R = res.results
    y_p = np.stack([R[b]["yp"] for b in range(N_CORES)])
    y_s = np.concatenate([R[b]["ys"].reshape(TS, NS, D).transpose(1, 0, 2) for b in range(N_CORES)])
    pool_p = np.stack([R[b]["o_pool_p"] for b in range(N_CORES)])[None]
    re_p = np.stack([R[b]["o_re_p"].reshape(32, 64) for b in range(N_CORES)])[None]
    im_p = np.stack([R[b]["o_im_p"].reshape(32, 64) for b in range(N_CORES)])[None]
    pool_s = np.concatenate([R[b]["o_pool_s"] for b in range(N_CORES)])[None]
    re_s = np.concatenate([R[b]["o_re_s"].reshape(NS, 32, 64) for b in range(N_CORES)])[None]
    im_s = np.concatenate([R[b]["o_im_s"].reshape(NS, 32, 64) for b in range(N_CORES)])[None]
    out = (y_p, y_s, pool_p, re_p, im_p, pool_s, re_s, im_s)
    return tuple(np.ascontiguousarray(o, dtype=np.float32) for o in out)
```

```python
import math
import numpy as np
import concourse.bass as bass
import concourse.mybir as mybir
from concourse.bass_utils import run_bass_kernel_spmd

F32 = mybir.dt.float32
BF16 = mybir.dt.bfloat16
I32 = mybir.dt.int32
ALU = mybir.AluOpType
AF = mybir.ActivationFunctionType

N_CORES = 8
D = 1024
SEQ = 2048
DFF = 2816
NFC = DFF // 128
NS = 16
TS = 4
HIST = 15
EPS = 1e-6
LP = 8
LS = 4
LMAX = 8
TP = 512
COMPUTE = ("pe", "act", "dve", "pool")
N_DSEM = 8
DEBUG = {}


class Op:
    __slots__ = ("eng", "emit", "waits", "tok", "is_dma", "name")


class Sched:
    def __init__(self, nc):
        self.nc = nc
        self.ops = {e: [] for e in COMPUTE + ("sp",)}
        self.sem = {e: nc.alloc_semaphore("sem_" + e) for e in COMPUTE}
        self.cnt = {e: 0 for e in COMPUTE}
        self.dsem = {q: [nc.alloc_semaphore(f"dsem_{q}{i}") for i in range(N_DSEM)]
                     for q in ("sp", "pool", "act")}
        self.dval = {q: [0] * N_DSEM for q in ("sp", "pool", "act")}
        self.dnext = {q: 0 for q in ("sp", "pool", "act")}
        self.last_w = {}
        self.readers = {}
        self.pending = {}
        self.known = {e: {} for e in COMPUTE + ("sp",)}
        self.nops = 0

    @staticmethod
    def _add(waits, tok):
        if tok is None:
            return
        s, v = tok
        k = id(s)
        if k not in waits or waits[k][1] < v:
            waits[k] = (s, v)

    def tokens_of(self, keys):
        out = []
        for k in keys:
            if k in self.last_w:
                out.append(self.last_w[k])
            out.extend(self.readers.get(k, ()))
            out.extend(self.pending.get(k, ()))
        return out

    def alias(self, new_keys, old_keys):
        toks = self.tokens_of(old_keys)
        for k in new_keys:
            self.pending.setdefault(k, []).extend(toks)

    def op(self, eng, emit, reads=(), writes=(), deps=(), dma=False, name=""):
        o = Op()
        o.eng, o.emit, o.is_dma, o.name = eng, emit, dma, name
        waits = {}
        for r in reads:
            self._add(waits, self.last_w.get(r))
        for w in writes:
            self._add(waits, self.last_w.get(w))
            for t in self.readers.get(w, ()):
                self._add(waits, t)
            for t in self.pending.pop(w, ()):
                self._add(waits, t)
        for d in deps:
            self._add(waits, d)
        if dma:
            j = self.dnext[eng]
            self.dnext[eng] = (j + 1) % N_DSEM
            s = self.dsem[eng][j]
            if self.dval[eng][j] > 0:
                self._add(waits, (s, self.dval[eng][j]))
            self.dval[eng][j] += 16
            o.tok = (s, self.dval[eng][j])
        else:
            self.cnt[eng] += 1
            o.tok = (self.sem[eng], self.cnt[eng])
        known = self.known[eng]
        wl = []
        for k, (s, v) in waits.items():
            if eng == "pe" and s is self.sem["pe"]:
                continue
            if known.get(k, 0) >= v:
                continue
            known[k] = v
            wl.append((s, v))
        o.waits = wl
        for r in reads:
            self.readers.setdefault(r, []).append(o.tok)
        for w in writes:
            self.last_w[w] = o.tok
            self.readers[w] = []
        self.ops[eng].append(o)
        self.nops += 1
        return o.tok

    def _replay(self, name, e):
        for o in self.ops[name]:
            for s, v in o.waits:
                e.wait_ge(s, v)
            ins = o.emit(e)
            ins.then_inc(o.tok[0], 16 if o.is_dma else 1)

    def emit_all(self, final_tokens):
        nc = self.nc
        best = {}
        for s, v in final_tokens:
            if id(s) not in best or best[id(s)][1] < v:
                best[id(s)] = (s, v)
        with nc.Block() as block:
            @block.tensor
            def _(e):
                self._replay("pe", e)

            @block.scalar
            def _(e):
                self._replay("act", e)

            @block.vector
            def _(e):
                self._replay("dve", e)

            @block.gpsimd
            def _(e):
                self._replay("pool", e)

            @block.sync
            def _(e):
                self._replay("sp", e)
                for s, v in best.values():
                    e.wait_ge(s, v)


def bc(ap, shape):
    return ap.broadcast_to(list(shape))


class Builder:
    def __init__(self, debug=(), stream_order=None, i_last0=None):
        self.debug = set(debug)
        self.stream_order = stream_order
        self.i_last0 = i_last0
        self.slot_last = {}
        self.sample_first_idx = None
        self.rec = []
        self.stream_pos = 0
        self.stream_issued = 0
        nc = self.nc = bass.Bass("TRN2", target_bir_lowering=False)
        self.S = Sched(nc)
        self.final = []
        self.dbg_out = {}
        self._uid = 0
        self.declare_io()
        self.alloc()

    def din(self, name, shape, dt=F32):
        return self.nc.dram_tensor(name, list(shape), dt, kind="ExternalInput")

    def dout(self, name, shape, dt=F32):
        return self.nc.dram_tensor(name, list(shape), dt, kind="ExternalOutput")

    def dscr(self, name, shape, dt=BF16):
        return self.nc.dram_tensor(name, list(shape), dt, kind="Internal")

    def sb(self, name, shape, dt=F32):
        return self.nc.alloc_sbuf_tensor(name, list(shape), dt).ap()

    def uid(self, p="k"):
        self._uid += 1
        return f"{p}{self._uid}"

    def dve(self, fn, reads=(), writes=(), **kw):
        return self.S.op("dve", fn, reads, writes, **kw)

    def act(self, fn, reads=(), writes=(), **kw):
        return self.S.op("act", fn, reads, writes, **kw)

    def pool(self, fn, reads=(), writes=(), **kw):
        return self.S.op("pool", fn, reads, writes, **kw)

    def pe(self, fn, reads=(), writes=(), **kw):
        return self.S.op("pe", fn, reads, writes, **kw)

    def load(self, out, in_, reads=(), writes=(), slow=False, q="sp", **kw):
        if slow:
            fn = lambda e: e.dma_start(out=out, in_=in_, allow_slow_non_contiguous=True)
        else:
            fn = lambda e: e.dma_start(out=out, in_=in_)
        return self.S.op(q, fn, reads, writes, dma=True, **kw)

    def store(self, out, in_, reads=(), writes=(), final=True, **kw):
        t = self.S.op("pool", lambda e: e.dma_start(out=out, in_=in_), reads, writes, dma=True, **kw)
        if final:
            self.final.append(t)
        return t

    def dbg(self, name, ap, key, shape):
        if name not in self.debug:
            return
        o = self.dout("dbg_" + name, shape, ap.dtype)
        self.dbg_out[name] = o
        self.store(o.ap(), ap, reads=[key])

    def declare_io(self):
        d = self.din
        self.xp = d("xp", [SEQ, D])
        self.xs = d("xs", [NS * TS, D])
        self.spool = d("spool", [NS, HIST, 512])
        self.sre = d("sre", [NS, 2048])
        self.sim = d("sim", [NS, 2048])
        self.norm_mix = d("norm_mix", [D])
        self.w_in = d("w_in", [D, 3072])
        self.pool_w = d("pool_w", [4, 128, 128])
        self.pool_scale = d("pool_scale", [512])
        self.a_re = d("ssm_a_re", [32 * 64])
        self.a_im = d("ssm_a_im", [32 * 64])
        self.log_dt = d("ssm_log_dt", [32])
        self.b_re = d("ssm_b_re", [32 * 64 * 16])
        self.b_im = d("ssm_b_im", [32 * 64 * 16])
        self.c_re = d("ssm_c_re", [32 * 16 * 64])
        self.c_im = d("ssm_c_im", [32 * 16 * 64])
        self.ssm_d = d("ssm_d", [512])
        self.glu_w = d("glu_w", [512, 512])
        self.glu_b = d("glu_b", [512])
        self.wbp = d("w_branch_pool", [512, D])
        self.wbs = d("w_branch_ssm", [512, D])
        self.w_out = d("w_out", [D, D])
        self.norm_ffn = d("norm_ffn", [D])
        self.wg = d("ffn_w_gate", [D, DFF])
        self.wu = d("ffn_w_up", [D, DFF])
        self.wd = d("ffn_w_down", [DFF, D])
        self.norm_final = d("norm_final", [D])
        o = self.dout
        self.yp = o("yp", [SEQ, D])
        self.ys = o("ys", [NS * TS, D])
        self.o_pool_p = o("o_pool_p", [HIST, 512])
        self.o_re_p = o("o_re_p", [16, 128])
        self.o_im_p = o("o_im_p", [16, 128])
        self.o_pool_s = o("o_pool_s", [NS, HIST, 512])
        self.o_re_s = o("o_re_s", [NS, 2048])
        self.o_im_s = o("o_im_s", [NS, 2048])
        s = self.dscr
        self.s_win = s("s_win", [6, 128, 8, 512])
        self.s_pw = s("s_pw", [128, 4, 128])
        self.s_glu = s("s_glu", [128, 4, 512])
        self.s_wbp = s("s_wbp", [128, 4, 1024])
        self.s_wbs = s("s_wbs", [128, 4, 1024])
        self.s_wout = s("s_wout", [2, 128, 8, 512])
        self.s_wg = s("s_wg", [11, 128, 8, 256])
        self.s_wu = s("s_wu", [11, 128, 8, 256])
        self.s_wd = s("s_wd", [NFC, 128, 1024])

    def alloc(self):
        sb = self.sb
        nc = self.nc
        self.ps = [nc.alloc_psum_tensor(f"ps{i}", [128, 512], F32).ap() for i in range(8)]
        self.psk = [[f"ps{i}"] for i in range(8)]
        self.ps_rrd = {}
        self.bpool = "all"
        self.ident_f = sb("ident_f", [128, 128])
        self.ident_b = sb("ident_b", [128, 128], BF16)
        self.maskS = sb("maskS", [128, 2])
        self.maskQ = sb("maskQ", [128, 4])
        self.halfpi = sb("halfpi", [128, 1])
        self.invc = sb("invc", [128, 16])
        self.gm = sb("gm", [128, 8])
        self.gn = sb("gn", [128, 8])
        self.gf = sb("gf", [128, D])
        self.pscale = sb("pscale", [128, 4])
        self.dsk = sb("dsk", [128, 4])
        self.hgb = sb("hgb", [128, 4])
        self.pw = sb("pw", [128, 4, 128], BF16)
        self.glu = sb("glu", [128, 4, 512], BF16)
        self.wbp_sb = sb("wbp_sb", [128, 4, 1024], BF16)
        self.wbs_sb = sb("wbs_sb", [128, 4, 1024], BF16)
        self.XW = sb("XW", [128, 4, LMAX, 2, 128], BF16)
        self.CW = sb("CW", [128, 16, LMAX, 2, 32], BF16)
        self.KW = sb("KW", [128, 4, LMAX, 128], BF16)
        self.Mrr = {L: sb(f"Mrr{L}", [128, 2, 16]) for L in (LS, LP)}
        self.Mii = {L: sb(f"Mii{L}", [128, 2, 16]) for L in (LS, LP)}
        self.NRING = 3
        self.ring = [sb(f"ring{i}", [128, 4096], BF16) for i in range(self.NRING)]
        self.xt = [sb(f"xt{i}", [128, 4, D]) for i in range(2)]
        x1 = self.xt[1].rearrange("p a b -> p (a b)").bitcast(BF16)
        self.ring += [x1[:, 0:4096], x1[:, 4096:8192]]
        self.ringk = {0: ["ring0a", "ring0b"], 1: ["ring1a", "ring1b"], 2: ["ring2a", "ring2b"],
                      3: ["xt1_0", "xt1_1"], 4: ["xt1_2", "xt1_3"]}
        self.xb = [[sb(f"xb{s}{i}", [128, D], BF16) for i in range(2 - s)] for s in range(2)]
        self.ss = [sb(f"ss{s}", [128, 4]) for s in range(2)]
        self.rs = [sb(f"rs{s}", [128, 4]) for s in range(2)]
        self.rtmp = [[sb(f"rtmp{s}{i}", [128, 4]) for i in range(3)] for s in range(2)]
        self.xnT = sb("xnT", [128, 8, TP], BF16)
        self.hnT = sb("hnT", [128, 8, TP], BF16)
        self.arB = sb("arB", [128, 6144 + NFC * 512], BF16)
        self.fA = sb("fA", [128, 2112])
        self.fB = sb("fB", [128, 1056])
        self.fX = sb("fX", [128, 2048])
        self.fC = sb("fC", [128, 3, TP])
        self.stash = sb("stash", [128, 4, HIST])
        self.carry = sb("carry", [128, 2, 16])
        self.sct = [sb(f"sct{i}", [128, 2, 16]) for i in range(4)]
        self.msm = sb("msm", [128, 1, 128])
        xf = self.xt[0].rearrange("p a b -> p (a b)")
        self.sst = [xf[0:NS, 1024:3072], xf[0:NS, 1024:3072]]
        self.sstk = [["xt0_1", "xt0_2"], ["xt0_1", "xt0_2"]]

    def bank(self):
        pool = {"A": [4, 5], "B": [0, 1, 2, 3], "all": [0, 1, 2, 3, 4, 5]}[self.bpool]
        i = self.ps_rrd.get(self.bpool, 0)
        self.ps_rrd[self.bpool] = (i + 1) % len(pool)
        return pool[i]

    def setup_consts(self):
        P = self.pool
        D_ = self.dve
        idf, idb = self.ident_f, self.ident_b
        P(lambda e: e.memset(idf, 1.0), writes=["ident_f"])
        P(lambda e: e.affine_select(out=idf, in_=idf, compare_op=ALU.is_equal, fill=0.0, base=0,
                                    pattern=[[-1, 128]], channel_multiplier=1),
          reads=["ident_f"], writes=["ident_f"])
        D_(lambda e: e.tensor_copy(out=idb, in_=idf), reads=["ident_f"], writes=["ident_b"])
        mS, mQ = self.maskS, self.maskQ
        P(lambda e: e.memset(mS, 0.0), writes=["maskS"])
        P(lambda e: e.memset(mS[0:64, 0:1], 1.0), writes=["maskS"])
        P(lambda e: e.memset(mS[64:128, 1:2], 1.0), writes=["maskS"])
        P(lambda e: e.memset(mQ, 0.0), writes=["maskQ"])
        for q in range(4):
            P(lambda e, q=q: e.memset(mQ[32 * q:32 * q + 32, q:q + 1], 1.0), writes=["maskQ"])
        P(lambda e: e.memset(self.halfpi, math.pi / 2), writes=["halfpi"])
        P(lambda e: e.iota(self.invc, [[1, 16]], base=1, channel_multiplier=0,
                           allow_small_or_imprecise_dtypes=True), writes=["invc"])
        D_(lambda e: e.reciprocal(out=self.invc, in_=self.invc), reads=["invc"], writes=["invc"])
        P(lambda e: e.memset(self.stash, 0.0), writes=["stash"])
        P(lambda e: e.memset(self.carry, 0.0), writes=["carry"])
        L = self.load

    def load_small_vectors(self):
        L = self.load
        D_ = self.dve
        L(self.gn, self.norm_ffn.ap().rearrange("(k p) -> p k", p=128), writes=["gn"], slow=True, q="act")
        L(self.gf, bass.AP(self.norm_final, 0, [[0, 128], [1, D]]), writes=["gf"])
        L(self.pscale, self.pool_scale.ap().rearrange("(k p) -> p k", p=128), writes=["pscale"], slow=True, q="act")
        L(self.dsk, self.ssm_d.ap().rearrange("(k p) -> p k", p=128), writes=["dsk"], slow=True, q="act")
        L(self.hgb, self.glu_b.ap().rearrange("(k p) -> p k", p=128), writes=["hgb"], slow=True, q="act")
        D_(lambda e: e.tensor_scalar(out=self.hgb, in0=self.hgb, scalar1=0.5, scalar2=None, op0=ALU.mult),
           reads=["hgb"], writes=["hgb"])


    def convert_weights(self):
        def cv(out, in_, key):
            self.S.op("pool", lambda e: e.dma_start(out=out, in_=in_), writes=[key], dma=True)
        win = self.w_in.ap().rearrange("(k p) n -> p k n", p=128)
        for c in range(6):
            cv(self.s_win.ap()[c], win[:, :, c * 512:(c + 1) * 512], f"s_win{c}")
        cv(self.s_pw.ap(), self.pool_w.ap().rearrange("g c d -> c g d"), "s_pw")
        cv(self.s_glu.ap(), self.glu_w.ap().rearrange("(k p) n -> p k n", p=128), "s_glu")
        cv(self.s_wbp.ap(), self.wbp.ap().rearrange("(k p) n -> p k n", p=128), "s_wbp")
        cv(self.s_wbs.ap(), self.wbs.ap().rearrange("(k p) n -> p k n", p=128), "s_wbs")
        wo = self.w_out.ap().rearrange("(k p) n -> p k n", p=128)
        for h in range(2):
            cv(self.s_wout.ap()[h], wo[:, :, h * 512:(h + 1) * 512], f"s_wout{h}")

    def convert_weights_ffn(self):
        def cv(out, in_, key):
            self.S.op("pool", lambda e: e.dma_start(out=out, in_=in_), writes=[key], dma=True)
        wg = self.wg.ap().rearrange("(k p) n -> p k n", p=128)
        wu = self.wu.ap().rearrange("(k p) n -> p k n", p=128)
        for j in range(11):
            cv(self.s_wg.ap()[j], wg[:, :, j * 256:(j + 1) * 256], f"s_wg{j}")
            cv(self.s_wu.ap()[j], wu[:, :, j * 256:(j + 1) * 256], f"s_wu{j}")
            yield
        wd = self.wd.ap().rearrange("(f p) n -> f p n", p=128)
        for f0 in range(0, NFC, 2):
            cv(self.s_wd.ap()[f0:f0 + 2], wd[f0:f0 + 2], f"s_wd{f0 // 2}")
            if f0 % 4 == 2:
                yield

    def load_resident(self):
        self.load(self.pw, self.s_pw.ap(), reads=["s_pw"], writes=["pw"])
        self.load(self.glu, self.s_glu.ap(), reads=["s_glu"], writes=["glu"])
        self.load(self.wbp_sb, self.s_wbp.ap(), reads=["s_wbp"], writes=["wbp"])
        self.load(self.wbs_sb, self.s_wbs.ap(), reads=["s_wbs"], writes=["wbs"])

    def wspec(self, name):
        if name.startswith("win"):
            c = int(name[3:])
            return [(lambda r: r.rearrange("p (k n) -> p k n", k=8), self.s_win.ap()[c], [f"s_win{c}"])]
        if name.startswith("wout"):
            h = int(name[4:])
            return [(lambda r: r.rearrange("p (k n) -> p k n", k=8), self.s_wout.ap()[h], [f"s_wout{h}"])]
        if name.startswith("gu"):
            j = int(name[2:])
            return [(lambda r: r[:, 0:2048].rearrange("p (k n) -> p k n", k=8), self.s_wg.ap()[j], [f"s_wg{j}"]),
                    (lambda r: r[:, 2048:4096].rearrange("p (k n) -> p k n", k=8), self.s_wu.ap()[j], [f"s_wu{j}"])]
        if name.startswith("wd"):
            f0 = int(name[2:])
            nf = min(4, NFC - f0)
            return [(lambda r, nf=nf: r[:, 0:nf * 1024].rearrange("p (f n) -> p f n", f=nf),
                     self.s_wd.ap()[f0:f0 + nf].rearrange("f p n -> p f n"),
                     [f"s_wd{(f0 + i) // 2}" for i in range(0, nf, 2)])]
        raise KeyError(name)

    def slot_of(self, i):
        if self.i_last0 is None or i < self.i_last0 + 3:
            return i % self.NRING
        return [3, 4, 0, 1, 2][(i - self.i_last0 - 3) % 5]

    def prefetch(self, upto):
        if self.stream_order is None:
            return
        upto = min(upto, len(self.stream_order))
        consumed = self.stream_pos - 1
        while self.stream_issued < upto:
            i = self.stream_issued
            slot = self.slot_of(i)
            prev = self.slot_last.get(slot, -1)
            if prev >= 0 and prev >= consumed:
                break
            parts = self.wspec(self.stream_order[i])
            ks = self.ringk[slot]
            for pi, (vf, src, skeys) in enumerate(parts):
                wk = ks if len(parts) == 1 else [ks[pi]]
                self.load(vf(self.ring[slot]), src, reads=skeys, writes=wk)
            self.slot_last[slot] = i
            self.stream_issued += 1

    def next_w(self, expect, nopref=False):
        i = self.stream_pos
        self.stream_pos += 1
        if self.stream_order is None:
            self.rec.append(expect)
        else:
            assert self.stream_order[i] == expect, (i, self.stream_order[i], expect)
            depth = 5 if (self.i_last0 is not None and i >= self.i_last0) else self.NRING
            if not nopref:
                self.prefetch(i + depth)
            assert self.stream_issued > i, (i, self.stream_issued)
        slot = self.slot_of(i)
        return self.ring[slot], list(self.ringk[slot])

    def scr(self, name, shape, pool=1, dt=F32):
        n = int(np.prod(shape[1:]))
        if dt == BF16:
            n = (n + 1) // 2
        st = self.scr_state[pool]
        while True:
            buf = st["bufs"][st["i"]]
            if st["off"] + n <= buf.shape[1]:
                break
            st["i"] += 1
            st["off"] = 0
        ap = buf[:, st["off"]:st["off"] + n]
        st["off"] += n
        self.setup_keys[pool].append(name)
        if dt == BF16:
            ap = ap.bitcast(BF16)[:, 0:int(np.prod(shape[1:]))]
        if len(shape) == 3:
            ap = ap.rearrange("p (a b) -> p a b", b=shape[2])
        return ap

    def cmul(self, eng, ore, oim, are, aim, bre, bim, t1, t2, rk, wk, tk):
        E = lambda fn, r, w: self.S.op(eng, fn, r, w)
        TT = lambda o, a, b, op: (lambda e: e.tensor_tensor(out=o, in0=a, in1=b, op=op))
        E(TT(t1, are, bre, ALU.mult), rk, [tk + "1"])
        E(TT(t2, aim, bim, ALU.mult), rk, [tk + "2"])
        E(TT(ore, t1, t2, ALU.subtract), [tk + "1", tk + "2"], wk)
        E(TT(t1, are, bim, ALU.mult), rk, [tk + "1"])
        E(TT(t2, aim, bre, ALU.mult), rk, [tk + "2"])
        E(TT(oim, t1, t2, ALU.add), [tk + "1", tk + "2"], wk)

    def discretise(self, are, aim, dtb):
        names = ("x1", "mag", "ang", "nn", "rr", "sn", "cs", "ar", "abre", "abim", "cre", "cim", "t1", "t2", "den", "nre")
        T = {n: self.scr("S_" + n, [128, 16]) for n in names}
        K = lambda s: "S_" + s
        E = lambda fn, r, w: self.S.op("dve", fn, [K(x) for x in r], [K(x) for x in w])
        A = lambda fn, r, w: self.S.op("act", fn, [K(x) for x in r], [K(x) for x in w])
        TT = lambda o, a, b, op: (lambda e: e.tensor_tensor(out=T[o] if isinstance(o, str) else o,
                                                            in0=T[a] if isinstance(a, str) else a,
                                                            in1=T[b] if isinstance(b, str) else b, op=op))
        E(TT("x1", are, dtb, ALU.mult), ["are", "dt"], ["x1"])
        A(lambda e: e.activation(out=T["mag"], in_=T["x1"], func=AF.Exp), ["x1"], ["mag"])
        E(TT("ang", aim, dtb, ALU.mult), ["aim", "dt"], ["ang"])
        E(lambda e: e.tensor_scalar(out=T["nn"], in0=T["ang"], scalar1=math.pi, scalar2=None, op0=ALU.is_gt), ["ang"], ["nn"])
        for m in range(2, 9):
            E(lambda e, m=m: e.scalar_tensor_tensor(out=T["nn"], in0=T["ang"], scalar=(2 * m - 1) * math.pi, in1=T["nn"],
                                                    op0=ALU.is_gt, op1=ALU.add), ["ang", "nn"], ["nn"])
        E(lambda e: e.scalar_tensor_tensor(out=T["rr"], in0=T["nn"], scalar=-2.0 * math.pi, in1=T["ang"],
                                           op0=ALU.mult, op1=ALU.add), ["nn", "ang"], ["rr"])
        A(lambda e: e.activation(out=T["sn"], in_=T["rr"], func=AF.Sin), ["rr"], ["sn"])
        E(lambda e: e.tensor_scalar(out=T["ar"], in0=T["rr"], scalar1=-1.0, scalar2=None, op0=ALU.mult), ["rr"], ["ar"])
        E(TT("ar", "ar", "rr", ALU.max), ["ar", "rr"], ["ar"])
        self.S.op("act", lambda e: e.activation(out=T["cs"], in_=T["ar"], func=AF.Sin, scale=-1.0, bias=self.halfpi[:, 0:1]),
                  ["S_ar", "halfpi"], ["S_cs"])
        E(TT("abre", "mag", "cs", ALU.mult), ["mag", "cs"], ["abre"])
        E(TT("abim", "mag", "sn", ALU.mult), ["mag", "sn"], ["abim"])
        E(TT("t1", are, are, ALU.mult), ["are"], ["t1"])
        E(TT("t2", aim, aim, ALU.mult), ["aim"], ["t2"])
        E(TT("den", "t1", "t2", ALU.add), ["t1", "t2"], ["den"])
        E(lambda e: e.reciprocal(out=T["den"], in_=T["den"]), ["den"], ["den"])
        E(lambda e: e.tensor_scalar(out=T["nre"], in0=T["abre"], scalar1=-1.0, scalar2=None, op0=ALU.add), ["abre"], ["nre"])
        E(TT("t1", "nre", are, ALU.mult), ["nre", "are"], ["t1"])
        E(TT("t2", "abim", aim, ALU.mult), ["abim", "aim"], ["t2"])
        E(TT("t1", "t1", "t2", ALU.add), ["t1", "t2"], ["t1"])
        E(TT("cre", "t1", "den", ALU.mult), ["t1", "den"], ["cre"])
        E(TT("t1", "abim", are, ALU.mult), ["abim", "are"], ["t1"])
        E(TT("t2", "nre", aim, ALU.mult), ["nre", "aim"], ["t2"])
        E(TT("t1", "t1", "t2", ALU.subtract), ["t1", "t2"], ["t1"])
        E(TT("cim", "t1", "den", ALU.mult), ["t1", "den"], ["cim"])
        return T["abre"], T["abim"], T["cre"], T["cim"]

    def setup_s5_p1(self):
        S = self.S
        L = self.load
        scr = self.scr
        flat = lambda x: x.rearrange("p a b -> p (a b)")
        self.scr_state = {
            1: {"bufs": [self.fA, self.fX, self.fB, self.arB[:, 0:6144].bitcast(F32)], "i": 0, "off": 0},
            2: {"bufs": [flat(self.xt[1]), self.arB[:, 6144:6144 + NFC * 512].bitcast(F32), flat(self.hnT).bitcast(F32),
                         flat(self.fC)], "i": 0, "off": 0},
        }
        self.setup_keys = {1: [], 2: []}
        prm = scr("S_prm", [128, 3, 16])
        areS, aimS, ldtS = prm[:, 0, :], prm[:, 1, :], prm[:, 2, :]
        dtS = scr("S_dt", [128, 16])
        stg = scr("S_stg", [128, 384])
        ld2 = scr("S_ld2", [128, 2])
        L(stg[0:16, 0:128], bass.AP(self.a_re, 0, [[128, 16], [1, 128]]), writes=["S_stg0"])
        L(stg[0:16, 128:256], bass.AP(self.a_im, 0, [[128, 16], [1, 128]]), writes=["S_stg1"], q="act")
        L(ld2[0:16, :], bass.AP(self.log_dt, 0, [[2, 16], [1, 2]]), writes=["S_ld2"])
        self.setup_keys[1] += ["S_stg0", "S_stg1", "S_stg2"]
        self.dve(lambda e: e.tensor_copy(out=stg[0:16, 256:384].rearrange("p (a c) -> p a c", a=2),
                                         in_=bc(ld2[0:16, :].unsqueeze(2), [16, 2, 64])), ["S_ld2"], ["S_stg2"])
        b = self.bank()
        def trp(e, b=b):
            for j in range(3):
                ins = e.transpose(out=self.ps[b][:, j * 16:(j + 1) * 16], in_=stg[0:16, j * 128:(j + 1) * 128],
                                  identity=self.ident_f[0:16, 0:16])
            return ins
        self.pe(trp, ["S_stg0", "S_stg1", "S_stg2", "ident_f"], self.psk[b])
        self.dve(lambda e, b=b: e.tensor_copy(out=prm.rearrange("p j s -> p (j s)"), in_=self.ps[b][:, 0:48]),
                 self.psk[b], ["S_are", "S_aim", "S_ldt"])
        self.setup_keys[1] += ["S_are", "S_aim", "S_ldt"]
        self.act(lambda e: e.activation(out=dtS, in_=ldtS, func=AF.Exp), ["S_ldt"], ["S_dt"])
        abreS, abimS, creS, cimS = self.discretise(areS, aimS, dtS)
        PS_re, PS_im = scr("PS_re", [128, 9, 16], 2), scr("PS_im", [128, 9, 16], 2)
        self.PS_re, self.PS_im = PS_re, PS_im
        tS1, tS2 = scr("tS1", [128, 16]), scr("tS2", [128, 16])
        self.setup_keys[2] += [f"PSre{k}" for k in range(9)] + [f"PSim{k}" for k in range(9)]
        self.dve(lambda e: e.tensor_copy(out=PS_re[:, 1, :], in_=abreS), ["S_abre"], ["PSre1"])
        self.dve(lambda e: e.tensor_copy(out=PS_im[:, 1, :], in_=abimS), ["S_abim"], ["PSim1"])
        for k in range(1, 8):
            self.cmul("dve", PS_re[:, k + 1, :], PS_im[:, k + 1, :], PS_re[:, k, :], PS_im[:, k, :], abreS, abimS,
                      tS1, tS2, [f"PSre{k}", f"PSim{k}", "S_abre", "S_abim"], [f"PSre{k + 1}", f"PSim{k + 1}"], "tS")
        for L_ in (LS, LP):
            mrr, mii = self.Mrr[L_], self.Mii[L_]
            mk = f"M{L_}"
            self.dve(lambda e, mrr=mrr, L_=L_: e.tensor_copy(out=mrr, in_=bc(PS_re[:, L_:L_ + 1, :], [128, 2, 16])),
                     [f"PSre{L_}"], [mk])
            self.dve(lambda e, mii=mii, L_=L_: e.tensor_scalar(out=mii[:, 0, :], in0=PS_im[:, L_, :], scalar1=-1.0,
                                                              scalar2=None, op0=ALU.mult), [f"PSim{L_}", mk], [mk])
            self.dve(lambda e, mii=mii, L_=L_: e.tensor_copy(out=mii[:, 1, :], in_=PS_im[:, L_, :]),
                     [f"PSim{L_}", mk], [mk])
        BSr, BSi = scr("BSr", [128, 16, 16]), scr("BSi", [128, 16, 16])
        L(BSr, bass.AP(self.b_re, 0, [[16, 128], [2048, 16], [1, 16]]), writes=["BSr"])
        L(BSi, bass.AP(self.b_im, 0, [[16, 128], [2048, 16], [1, 16]]), writes=["BSi"])
        Bk = [[scr(f"Bk{i}r", [128, 16, 16]), scr(f"Bk{i}i", [128, 16, 16])] for i in range(2)]
        tB1, tB2 = scr("tB1", [128, 16, 16]), scr("tB2", [128, 16, 16])
        cb = lambda x: bc(x.unsqueeze(2), [128, 16, 16])
        self.cmul("dve", Bk[0][0], Bk[0][1], cb(creS), cb(cimS), BSr, BSi, tB1, tB2,
                  ["S_cre", "S_cim", "BSr", "BSi"], ["Bk0r", "Bk0i"], "tB")
        self.ZB = ZB = [scr("ZBr", [128, 16, 128], 2, BF16), scr("ZBi", [128, 16, 128], 2, BF16)]
        for ri in range(2):
            zk = "ZBr" if ri == 0 else "ZBi"
            bk = "Bk0r" if ri == 0 else "Bk0i"
            Z = ZB[ri]
            self.dve(lambda e, Z=Z: e.memset(Z, 0.0), [], [zk])
            Zv = Z.rearrange("p (f q) (u a c) -> p f q u a c", q=4, u=4, a=2)
            Bv = Bk[0][ri].rearrange("p (f q) c -> p f q c", q=4)
            for q in range(4):
                for a in range(2):
                    sl = slice(64 * a, 64 * a + 64)
                    self.dve(lambda e, Zv=Zv, Bv=Bv, q=q, a=a, sl=sl: e.tensor_copy(
                        out=Zv[sl, :, q, q, a, :], in_=Bv[sl, :, q, :]), [bk], [zk])
        Zraw = [scr(f"Zraw{i}", [128, 4, 128], 1, BF16) for i in range(2)]
        for i in range(2):
            self.dve(lambda e, i=i: e.memset(Zraw[i], 0.0), [], [f"Zraw{i}"])
        nz = 0
        for k in range(8):
            cur = Bk[k % 2]
            ck = [f"Bk{k % 2}r", f"Bk{k % 2}i"]
            if k > 0:
                prv = Bk[(k - 1) % 2]
                pk = [f"Bk{(k - 1) % 2}r", f"Bk{(k - 1) % 2}i"]
                self.cmul("dve", cur[0], cur[1], cb(abreS), cb(abimS), prv[0], prv[1], tB1, tB2,
                          ["S_abre", "S_abim"] + pk, ck, "tB")
            for ri in range(2):
                Z = Zraw[nz % 2]
                zk = f"Zraw{nz % 2}"
                nz += 1
                Zv = Z.rearrange("p f (q a c) -> p f q a c", q=4, a=2)
                Bv = cur[ri].rearrange("p (f q) c -> p f q c", q=4)
                for a in range(2):
                    sl = slice(64 * a, 64 * a + 64)
                    self.dve(lambda e, Zv=Zv, Bv=Bv, a=a, sl=sl: e.tensor_copy(out=Zv[sl, :, :, a, :], in_=Bv[sl]),
                             [ck[ri]], [zk])
                b = self.bank()
                pbv = self.ps[b].bitcast(BF16)
                def tr(e, pbv=pbv, Z=Z):
                    for ft in range(4):
                        ins = e.transpose(out=pbv[:, ft * 128:(ft + 1) * 128], in_=Z[:, ft, :], identity=self.ident_b)
                    return ins
                self.pe(tr, [zk, "ident_b"], self.psk[b])
                self.act(lambda e, pbv=pbv, k=k, ri=ri: e.activation(out=self.XW[:, :, 7 - k, ri, :],
                                                                     in_=pbv[:, 0:512].rearrange("p (f c) -> p f c", f=4),
                                                                     func=AF.Copy), self.psk[b], ["XW"])

    def setup_s5_p2(self):
        L = self.load
        scr = lambda n, s, dt=F32: self.scr(n, s, 2, dt)
        PS_re, PS_im, ZB = self.PS_re, self.PS_im, self.ZB
        self.bpool = "B"
        CS = [scr("CSr", [128, 16, 32]), scr("CSi", [128, 16, 32])]
        CSb = [scr("CSbr", [128, 16, 32], BF16), scr("CSbn", [128, 16, 32], BF16)]
        CT, ZC = scr("CT", [128, 4, 64]), scr("ZC", [128, 4, 128])
        for ri, src in enumerate((self.c_re, self.c_im)):
            ck = "CSr" if ri == 0 else "CSi"
            L(CT, bass.AP(src, 0, [[64, 128], [8192, 4], [1, 64]]), writes=["CT"])
            self.dve(lambda e: e.tensor_copy(out=ZC.rearrange("p f (a c) -> p f a c", a=2),
                                             in_=bc(CT.unsqueeze(2), [128, 4, 2, 64])), ["CT"], ["ZC"])
            b = self.bank()
            def trc(e, b=b):
                for ft in range(4):
                    ins = e.transpose(out=self.ps[b][:, ft * 128:(ft + 1) * 128], in_=ZC[:, ft, :], identity=self.ident_f)
                return ins
            self.pe(trc, ["ZC", "ident_f"], self.psk[b])
            self.dve(lambda e, b=b, ri=ri: e.tensor_tensor(
                out=CS[ri].rearrange("p s (a h) -> p s a h", a=2),
                in0=self.ps[b].rearrange("p (s a h) -> p s a h", a=2, h=16),
                in1=bc(self.maskS.unsqueeze(1).unsqueeze(3), [128, 16, 2, 16]), op=ALU.mult),
                self.psk[b] + ["maskS"], [ck])
        self.dve(lambda e: e.tensor_copy(out=CSb[0], in_=CS[0]), ["CSr"], ["CSbr"])
        self.dve(lambda e: e.tensor_scalar(out=CSb[1], in0=CS[1], scalar1=-1.0, scalar2=None, op0=ALU.mult), ["CSi"], ["CSbn"])
        yield
        tC1, tC2, cimt, clr = scr("tC1", [128, 16, 32]), scr("tC2", [128, 16, 32]), scr("cimt", [128, 16, 32]), scr("clr", [128, 16, 32])
        TT = lambda o, a, b, op: (lambda e: e.tensor_tensor(out=o, in0=a, in1=b, op=op))
        for k in range(1, 9):
            pr = bc(PS_re[:, k, :].unsqueeze(2), [128, 16, 32])
            pi_ = bc(PS_im[:, k, :].unsqueeze(2), [128, 16, 32])
            rk = ["CSr", "CSi", f"PSre{k}", f"PSim{k}"]
            self.dve(TT(tC1, CS[0], pr, ALU.mult), rk, ["tC1"])
            self.dve(TT(tC2, CS[1], pi_, ALU.mult), rk, ["tC2"])
            self.dve(TT(self.CW[:, :, k - 1, 0, :], tC1, tC2, ALU.subtract), ["tC1", "tC2"], [f"CW{k - 1}"])
            self.dve(TT(tC1, CS[0], pi_, ALU.mult), rk, ["tC1"])
            self.dve(TT(tC2, CS[1], pr, ALU.mult), rk, ["tC2"])
            self.dve(lambda e, k=k: e.scalar_tensor_tensor(out=self.CW[:, :, k - 1, 1, :], in0=tC1, scalar=-1.0, in1=tC2,
                                                           op0=ALU.mult, op1=ALU.subtract), ["tC1", "tC2"], [f"CW{k - 1}"])
        yield
        for k in range(8):
            self.bpool = "B"
            b = self.bank()
            def kmm(e, b=b, k=k):
                for ft in range(4):
                    n = 0
                    for q in range(4):
                        st = 4 * ft + q
                        for ri in range(2):
                            rhs = CSb[ri][:, st, :] if k == 0 else self.CW[:, st, k - 1, ri, :]
                            ins = e.matmul(self.ps[b][:, ft * 32:(ft + 1) * 32], lhsT=ZB[ri][:, st, :],
                                           rhs=rhs, start=(n == 0), stop=(n == 7))
                            n += 1
                return ins
            self.pe(kmm, ["ZBr", "ZBi"] + (["CSbr", "CSbn"] if k == 0 else [f"CW{k - 1}"]), self.psk[b])
            self.dve(lambda e, k=k, b=b: e.tensor_tensor(
                out=self.KW[:, :, k, :].rearrange("p f (u c) -> p f u c", u=4),
                in0=bc(self.ps[b][:, 0:128].rearrange("p (f c) -> p f c", f=4).unsqueeze(2), [128, 4, 4, 32]),
                in1=bc(self.maskQ.unsqueeze(1).unsqueeze(3), [128, 4, 4, 32]), op=ALU.mult),
                self.psk[b] + ["maskQ"], ["KW"])
        self.S.alias(["CW"], [f"CW{i}" for i in range(8)])
        self.act(lambda e: e.activation(out=self.CW[:, 0, 0, 0, 0:1], in_=self.CW[:, 0, 0, 0, 0:1], func=AF.Copy), ["CW0"], ["CW"])
        yield

    def rstd(self, ssv, out, n, rk, wk, st):
        v, y, t = self.rtmp[st]
        npart = ssv.shape[0]
        v, y, t = v[0:npart, 0:n], y[0:npart, 0:n], t[0:npart, 0:n]
        kv, ky, kt = f"rt_v{st}", f"rt_y{st}", f"rt_t{st}"
        Dv = self.dve
        Dv(lambda e: e.tensor_scalar(out=v, in0=ssv, scalar1=1.0 / D, scalar2=EPS, op0=ALU.mult, op1=ALU.add),
           rk, [kv])
        vi, yi = v.bitcast(I32), y.bitcast(I32)
        Dv(lambda e: e.tensor_single_scalar(out=yi, in_=vi, scalar=1, op=ALU.arith_shift_right), [kv], [ky])
        Dv(lambda e: e.tensor_scalar(out=yi, in0=yi, scalar1=-1.0, scalar2=1597463007.0, op0=ALU.mult, op1=ALU.add),
           [ky], [ky])
        for it in range(3):
            Dv(lambda e: e.tensor_tensor(out=t, in0=y, in1=y, op=ALU.mult), [ky], [kt])
            Dv(lambda e: e.tensor_tensor(out=t, in0=t, in1=v, op=ALU.mult), [kt, kv], [kt])
            Dv(lambda e: e.tensor_scalar(out=t, in0=t, scalar1=-0.5, scalar2=1.5, op0=ALU.mult, op1=ALU.add),
               [kt], [kt])
            o = out if it == 2 else y
            Dv(lambda e, o=o: e.tensor_tensor(out=o, in0=y, in1=t, op=ALU.mult), [ky, kt],
               wk if it == 2 else [ky])

    def sumsq(self, xt, xk, NB, npart, st):
        for nb in range(NB):
            nx = len(self.xb[0])
            xb = self.xb[0][nb % nx]
            self.act(lambda e, nb=nb, xb=xb: e.activation(out=xb[0:npart], in_=xt[0:npart, nb, :], func=AF.Square,
                                                         accum_out=self.ss[st][0:npart, nb:nb + 1]),
                     [xk[nb]], [f"xb0{nb % nx}", f"ss{st}_{nb}"])
        self.rstd(self.ss[st][0:npart, 0:NB], self.rs[st][0:npart, 0:NB], NB, [f"ss{st}_{nb}" for nb in range(NB)],
                  [f"rs{st}"], st)

    def norm_T(self, xt, xk, T, gam, gk, dstT, dk, st, do_sumsq=True):
        NB = (T + 127) // 128
        npart = min(T, 128)
        if do_sumsq:
            self.sumsq(xt, xk, NB, npart, st)
        for nb in range(NB):
            nx = len(self.xb[0])
            xb = self.xb[0][nb % nx]
            bk = f"xb0{nb % nx}"
            self.act(lambda e, nb=nb, xb=xb: e.activation(out=xb[0:npart], in_=xt[0:npart, nb, :], func=AF.Copy,
                                                         scale=self.rs[st][0:npart, nb:nb + 1]),
                     [xk[nb], f"rs{st}"], [bk])
            for half in range(2):
                pb = self.ps[6 + half].bitcast(BF16)
                hk = self.psk[6 + half][0]
                def tr(e, nb=nb, xb=xb, half=half, pb=pb):
                    for k4 in range(4):
                        kc = half * 4 + k4
                        ins = e.transpose(out=pb[:, k4 * 128: k4 * 128 + npart],
                                          in_=xb[0:npart, kc * 128:(kc + 1) * 128], identity=self.ident_b[0:npart, 0:npart])
                    return ins
                self.pe(tr, [bk, "ident_b"], [hk])
                src = pb[:, 0:512].rearrange("p (k t) -> p k t", k=4)[:, :, 0:npart]
                dst = dstT[:, half * 4:(half + 1) * 4, nb * 128: nb * 128 + npart]
                g = bc(gam[:, half * 4:(half + 1) * 4].unsqueeze(2), [128, 4, npart])
                self.dve(lambda e, src=src, dst=dst, g=g: e.tensor_tensor(out=dst, in0=src, in1=g, op=ALU.mult),
                         [hk, gk], [f"{dk}{half}_{nb}"])
        return [f"{dk}{h}_{nb}" for h in range(2) for nb in range(NB)]

    def mkctx(self, k, kind, t0, T, last):
        c = type("Ctx", (), {})()
        c.k, c.kind, c.t0, c.T, c.last = k, kind, t0, T, last
        c.NB = (T + 127) // 128
        c.npart = min(T, 128)
        c.xt = self.xt[k % 2]
        c.xk = [f"xt{k % 2}_{nb}" for nb in range(c.NB)]
        c.Lc = LP if kind == "p" else LS
        c.NC = T // c.Lc
        c.toff = LMAX - c.Lc
        A = self.arB
        v4 = lambda lo: A[:, lo:lo + 2048].rearrange("p (g t) -> p g t", g=4)[:, :, 0:T]
        c.diffb = v4(0)
        c.Sbf = A[:, 0:2048].rearrange("p (r s c) -> p r s c", r=2, s=16)[:, :, :, 0:c.NC]
        c.uT = v4(2048)
        c.aout = v4(4096)
        c.zbf = v4(6144)
        c.bo = v4(8192)
        c.mbf = A[:, 10240:14336].rearrange("p (g t) -> p g t", g=8)[:, :, 0:T]
        c.fbf = A[:, 6144:6144 + NFC * 512].rearrange("p (g t) -> p g t", g=NFC)[:, :, 0:T]
        c.b1keys = [f"zbf{g}" for g in range(4)] + [f"bo{g}" for g in range(4)] + [f"mbf{j}" for j in range(8)]
        c.fkeys = [f"f{i}" for i in range(NFC)]
        if kind == "p":
            c.colv = lambda ap2, i: ap2.rearrange("p (c l) -> p c l", l=c.Lc)[:, :, i]
            c.colrange = None
        else:
            c.colv = lambda ap2, i: ap2[:, i * NS:(i + 1) * NS]
            c.colrange = lambda ap2, lo, hi: ap2[:, lo * NS:hi * NS]
        return c

    def load_x(self, c):
        if getattr(c, "xloaded", False):
            return
        c.xloaded = True
        src = self.xp.ap() if c.kind == "p" else self.xs.ap()
        for nb in range(c.NB):
            self.load(c.xt[0:c.npart, nb, :], src[c.t0 + nb * 128: c.t0 + nb * 128 + c.npart, :], writes=[c.xk[nb]])

    def stageA(self, c):
        S = self.S
        k, kind, t0, T, last, NB, npart = c.k, c.kind, c.t0, c.T, c.last, c.NB, c.npart
        xt, xk, Lc, NC, toff = c.xt, c.xk, c.Lc, c.NC, c.toff
        diffb, Sbf, uT, aout, colv = c.diffb, c.Sbf, c.uT, c.aout, c.colv
        self.bpool = "A"
        self.load_x(c)
        xnT = self.xnT
        self.sumsq(xt, xk, NB, npart, 0)
        yield
        c.xnk = xnk = self.norm_T(xt, xk, T, self.gm, "gm", xnT, "xnT", 0, do_sumsq=False)
        yield
        self.bpool = "A"
        S.alias(["E"], ["S2"])
        S.alias([f"diff{g}" for g in range(4)], ["Sbf"])
        if kind == "p":
            E = self.fA[:, 0:4 * (HIST + T)].rearrange("p (g t) -> p g t", g=4)
            Enew = E[:, :, HIST:HIST + T]
            self.pool(lambda e: e.tensor_copy(out=E[:, :, 0:HIST], in_=self.stash), ["stash"], ["E"])
        else:
            E = self.fA[:, 0:4 * NS * 19].rearrange("p (g s t) -> p g s t", g=4, s=NS)
            self.sample_hist(E)
        ring, rk = self.next_w("win0")
        W = ring.rearrange("p (k n) -> p k n", k=8)
        for g in range(4):
            b = self.bank()
            def mm(e, g=g, b=b, W=W):
                for kc in range(8):
                    ins = e.matmul(self.ps[b][:, 0:T], lhsT=W[:, kc, g * 128:(g + 1) * 128], rhs=xnT[:, kc, 0:T],
                                   start=(kc == 0), stop=(kc == 7))
                return ins
            self.pe(mm, rk + xnk, self.psk[b])
            if kind == "p":
                self.act(lambda e, g=g, b=b: e.activation(out=Enew[:, g, :], in_=self.ps[b][:, 0:T], func=AF.Copy),
                         self.psk[b], ["E"])
            else:
                self.act(lambda e, g=g, b=b: e.activation(
                    out=E[:, g, :, HIST:HIST + TS].rearrange("p s t -> p t s"),
                    in_=self.ps[b][:, 0:T].rearrange("p (t s) -> p t s", s=NS), func=AF.Copy),
                    self.psk[b], ["E"])
        if kind == "s" or last:
            b = self.bank()
            nbl = NB - 1
            def mmt(e, b=b, W=W, nbl=nbl):
                for kc in range(8):
                    ins = e.matmul(self.ps[b][0:npart, :], lhsT=xnT[:, kc, nbl * 128: nbl * 128 + npart], rhs=W[:, kc, :],
                                   start=(kc == 0), stop=(kc == 7))
                return ins
            self.pe(mmt, rk + xnk, self.psk[b])
            ut = self.fC[0:npart, 2, :]
            self.dve(lambda e, b=b: e.tensor_copy(out=ut, in_=self.ps[b][0:npart, :]), self.psk[b], ["fC2"])
            if kind == "p":
                self.store(self.o_pool_p.ap(), ut[128 - HIST:128, :], reads=["fC2"])
            else:
                for t in range(TS):
                    self.store(self.o_pool_s.ap()[:, HIST - TS + t, :], ut[t * NS:(t + 1) * NS, :], reads=["fC2"])
        if kind == "p":
            self.pool(lambda e: e.tensor_copy(out=self.stash, in_=E[:, :, T:T + HIST]), ["E"], ["stash"])
        yield
        self.bpool = "A"
        if kind == "p":
            Ws = [self.fB[:, i * 528: i * 528 + HIST + T] for i in range(2)]
        for g in range(4):
            nlev = g + 1
            w = 2 ** nlev
            if kind == "p":
                ln = HIST + T
                cur = E[:, g, :]
                for lev in range(nlev):
                    sh = 2 ** lev
                    dstb = Ws[lev % 2]
                    self.pool(lambda e, cur=cur, dstb=dstb, sh=sh, ln=ln: e.tensor_tensor(
                        out=dstb[:, sh:ln], in0=cur[:, sh:ln], in1=cur[:, 0:ln - sh], op=ALU.add),
                        ["E", "PT"], ["PT"])
                    cur = dstb
                wsum = cur[:, HIST:HIST + T]
                sc = Ws[nlev % 2][:, 0:T]
                self.pool(lambda e, wsum=wsum, sc=sc, w=w: e.tensor_scalar(out=sc, in0=wsum, scalar1=1.0 / w, scalar2=None,
                                                                          op0=ALU.mult), ["PT"], ["PT"])
                if t0 == 0:
                    self.pool(lambda e, wsum=wsum, sc=sc, w=w: e.tensor_tensor(out=sc[:, 0:w - 1], in0=wsum[:, 0:w - 1],
                                                                              in1=self.invc[:, 0:w - 1], op=ALU.mult),
                              ["PT", "invc"], ["PT"])
                self.pool(lambda e, sc=sc, g=g: e.tensor_tensor(out=diffb[:, g, :], in0=sc, in1=Enew[:, g, :], op=ALU.subtract),
                          ["PT", "E"], [f"diff{g}"])
            else:
                ln = 19
                bufs = [self.fB[:, i * 320: i * 320 + NS * ln].rearrange("p (s t) -> p s t", s=NS) for i in range(2)]
                cur = E[:, g]
                for lev in range(nlev):
                    sh = 2 ** lev
                    dstb = bufs[lev % 2]
                    self.pool(lambda e, cur=cur, dstb=dstb, sh=sh: e.tensor_tensor(
                        out=dstb[:, :, sh:ln], in0=cur[:, :, sh:ln], in1=cur[:, :, 0:ln - sh], op=ALU.add),
                        ["E", "PT"], ["PT"])
                    cur = dstb
                wsum = cur[:, :, HIST:ln]
                sc = bufs[nlev % 2][:, :, 0:TS]
                self.pool(lambda e, wsum=wsum, sc=sc, w=w: e.tensor_scalar(out=sc, in0=wsum, scalar1=1.0 / w, scalar2=None,
                                                                          op0=ALU.mult), ["PT"], ["PT"])
                self.pool(lambda e, sc=sc, g=g: e.tensor_tensor(
                    out=diffb[:, g, :].rearrange("p (t s) -> p s t", s=NS), in0=sc, in1=E[:, g, :, HIST:ln], op=ALU.subtract),
                    ["PT", "E"], [f"diff{g}"])
        ring, rk = self.next_w("win1")
        W = ring.rearrange("p (k n) -> p k n", k=8)
        for ft in range(4):
            b = self.bank()
            def mm(e, ft=ft, b=b, W=W):
                for kc in range(8):
                    ins = e.matmul(self.ps[b][:, 0:T], lhsT=W[:, kc, ft * 128:(ft + 1) * 128], rhs=xnT[:, kc, 0:T],
                                   start=(kc == 0), stop=(kc == 7))
                return ins
            self.pe(mm, rk + xnk, self.psk[b])
            self.act(lambda e, ft=ft, b=b: e.activation(out=uT[:, ft, :], in_=self.ps[b][:, 0:T], func=AF.Copy),
                     self.psk[b], [f"uT{ft}"])
        yield
        self.bpool = "A"
        X2 = self.fX[:, 0:NC * 32].rearrange("p (c r s) -> p c r s", r=2, s=16)
        per_bank = 512 // NC if NC * 8 <= 512 else 8
        per_bank = min(per_bank, 8)
        for rnd in range(2):
            xb2 = [self.bank(), self.bank()]
            def xmm(e, rnd=rnd, xb2=xb2):
                for s8 in range(8):
                    st = rnd * 8 + s8
                    ft, q = st // 4, st % 4
                    for ri in range(2):
                        out = self.ps[xb2[ri]][:, s8 * NC:(s8 + 1) * NC]
                        for i in range(Lc):
                            ins = e.matmul(out, lhsT=self.XW[32 * q:32 * q + 32, ft, i + toff, ri, :],
                                           rhs=colv(uT[32 * q:32 * q + 32, ft, :], i),
                                           start=(i == 0), stop=(i == Lc - 1), tile_position=(32 * q, 0))
                return ins
            self.pe(xmm, ["XW"] + [f"uT{ft}" for ft in range(4)], self.psk[xb2[0]] + self.psk[xb2[1]])
            for ri in range(2):
                self.act(lambda e, rnd=rnd, ri=ri, xb2=xb2: e.activation(
                    out=X2[:, :, ri, rnd * 8:(rnd + 1) * 8].rearrange("p c s -> p s c"),
                    in_=self.ps[xb2[ri]][:, 0:8 * NC].rearrange("p (s c) -> p s c", c=NC), func=AF.Copy),
                    self.psk[xb2[ri]], ["X2"])
        yield
        self.bpool = "A"
        for g in range(4):
            b = self.bank()
            self.pe(lambda e, g=g, b=b: e.matmul(self.ps[b][:, 0:T], lhsT=self.pw[:, g, :], rhs=diffb[:, g, :],
                                                 start=True, stop=True), ["pw", f"diff{g}"], self.psk[b])
            self.act(lambda e, g=g, b=b: e.activation(out=aout[:, g, :], in_=self.ps[b][:, 0:T], func=AF.Copy,
                                                      scale=self.pscale[:, g:g + 1]), self.psk[b] + ["pscale"], [f"aout{g}"])
        yield
        S.alias(["S2"], ["E"])
        Mrr, Mii = self.Mrr[Lc], self.Mii[Lc]
        if kind == "p":
            S2 = self.fA[:, 0:(NC + 1) * 32].rearrange("p (c r s) -> p c r s", r=2, s=16)
            SC = self.dve if k == 0 else self.pool
            SC(lambda e: e.tensor_copy(out=S2[:, 0], in_=self.carry), ["carry"], ["S2"])
            for cc in range(NC):
                t1, t2 = self.sct[(2 * cc) % 4], self.sct[(2 * cc + 1) % 4]
                k1, k2 = f"sct{(2 * cc) % 4}", f"sct{(2 * cc + 1) % 4}"
                prev = S2[:, cc]
                prevsw = bass.AP(prev.tensor, prev.offset + 16, [list(prev.ap[0]), [-16, 2], [1, 16]])
                SC(lambda e, t1=t1, prev=prev: e.tensor_tensor(out=t1, in0=prev, in1=Mrr, op=ALU.mult),
                         ["S2", f"M{Lc}"], [k1])
                SC(lambda e, t2=t2, prevsw=prevsw: e.tensor_tensor(out=t2, in0=prevsw, in1=Mii, op=ALU.mult),
                         ["S2", f"M{Lc}"], [k2])
                SC(lambda e, t1=t1, t2=t2: e.tensor_tensor(out=t1, in0=t1, in1=t2, op=ALU.add), [k1, k2], [k1])
                SC(lambda e, t1=t1, cc=cc: e.tensor_tensor(out=S2[:, cc + 1], in0=t1, in1=X2[:, cc], op=ALU.add),
                         [k1, "X2"], ["S2"])
                if cc % 4 == 3:
                    yield
                    self.bpool = "A"
            SC(lambda e: e.tensor_copy(out=self.carry, in_=S2[:, NC]), ["S2"], ["carry"])
            Sprev = S2[:, 0:NC]
            Sfinal = None
        else:
            S2 = self.fA[:, 0:2 * NC * 32].rearrange("p (u c r s) -> p u c r s", u=2, r=2, s=16)
            self.sample_state_in(S2[:, 0])
            tA = self.fB[:, 0:NC * 32].rearrange("p (c r s) -> p c r s", r=2, s=16)
            tB = self.fB[:, 512:512 + NC * 32].rearrange("p (c r s) -> p c r s", r=2, s=16)
            S0 = S2[:, 0]
            S0sw = bass.AP(S0.tensor, S0.offset + 16, [list(S0.ap[0]), [32, NC], [-16, 2], [1, 16]])
            mrrb = bc(Mrr.unsqueeze(1), [128, NC, 2, 16])
            miib = bc(Mii.unsqueeze(1), [128, NC, 2, 16])
            self.dve(lambda e: e.tensor_tensor(out=tA, in0=S0, in1=mrrb, op=ALU.mult), ["S2", f"M{Lc}"], ["PT"])
            self.dve(lambda e: e.tensor_tensor(out=tB, in0=S0sw, in1=miib, op=ALU.mult), ["S2", f"M{Lc}", "PT"], ["PT"])
            self.dve(lambda e: e.tensor_tensor(out=tA, in0=tA, in1=tB, op=ALU.add), ["PT"], ["PT"])
            self.dve(lambda e: e.tensor_tensor(out=S2[:, 1], in0=tA, in1=X2, op=ALU.add), ["PT", "X2"], ["S2"])
            Sprev = S2[:, 0]
            Sfinal = S2[:, 1]
        S.alias(["Sbf"], [f"diff{g}" for g in range(4)])
        for ri in range(2):
            self.act(lambda e, ri=ri: e.activation(out=Sbf[:, ri], in_=Sprev[:, :, ri, :].rearrange("p c s -> p s c"),
                                                   func=AF.Copy), ["S2"], ["Sbf"])
        if kind == "s":
            self.sample_state_out(Sfinal)
        elif last:
            self.prompt_state_out()
        yield

    def stageB1(self, c):
        S = self.S
        k, kind, t0, T, last, NB, npart = c.k, c.kind, c.t0, c.T, c.last, c.NB, c.npart
        xt, xk, Lc, NC = c.xt, c.xk, c.Lc, c.NC
        Sbf, uT, aout, zbf, bo, mbf, colv, colrange = c.Sbf, c.uT, c.aout, c.zbf, c.bo, c.mbf, c.colv, c.colrange
        xnT, xnk = self.xnT, c.xnk
        self.bpool = "all"
        if kind == "s" and self.sample_first_idx is None:
            self.sample_first_idx = self.stream_pos
        S.alias(c.b1keys, c.fkeys)
        for ft in range(4):
            b = self.bank()
            psY = self.ps[b][:, 0:T]
            def ymm(e, ft=ft, psY=psY):
                ins = e.matmul(psY, lhsT=self.KW[:, ft, 0, :], rhs=uT[:, ft, :], start=True, stop=False)
                for j in range(1, Lc):
                    if kind == "p":
                        for i in range(j, Lc):
                            ins = e.matmul(colv(psY, i), lhsT=self.KW[:, ft, j, :], rhs=colv(uT[:, ft, :], i - j),
                                           start=False, stop=False)
                    else:
                        ins = e.matmul(colrange(psY, j, Lc), lhsT=self.KW[:, ft, j, :], rhs=colrange(uT[:, ft, :], 0, Lc - j),
                                       start=False, stop=False)
                for ri in range(2):
                    for i in range(Lc):
                        for q in range(4):
                            st = 4 * ft + q
                            ins = e.matmul(colv(psY[32 * q:32 * q + 32, :], i), lhsT=self.CW[:, st, i, ri, :],
                                           rhs=Sbf[:, ri, st, :], start=False, stop=(ri == 1 and i == Lc - 1),
                                           tile_position=(0, 32 * q))
                return ins
            self.pe(ymm, ["KW", "CW", "Sbf", f"uT{ft}"], self.psk[b])
            yf = self.fC[:, ft % 2, 0:T]
            self.dve(lambda e, ft=ft, psY=psY, yf=yf: e.scalar_tensor_tensor(out=yf, in0=uT[:, ft, :], scalar=self.dsk[:, ft:ft + 1],
                                                                            in1=psY, op0=ALU.mult, op1=ALU.add),
                     self.psk[b] + [f"uT{ft}", "dsk"], [f"fC{ft % 2}"])
            self.act(lambda e, ft=ft, yf=yf: e.activation(out=zbf[:, ft, :], in_=yf, func=AF.Gelu_apprx_tanh),
                     [f"fC{ft % 2}"], [f"zbf{ft}"])
        for fo in range(4):
            b = self.bank()
            def gmm(e, fo=fo, b=b):
                for kc in range(4):
                    ins = e.matmul(self.ps[b][:, 0:T], lhsT=self.glu[:, kc, fo * 128:(fo + 1) * 128], rhs=zbf[:, kc, :],
                                   start=(kc == 0), stop=(kc == 3))
                return ins
            self.pe(gmm, ["glu"] + [f"zbf{ft}" for ft in range(4)], self.psk[b])
            thl = self.fC[:, 2, 0:T] if fo % 2 == 0 else self.fC[:, 1, 0:T]
            tk_ = "fC2" if fo % 2 == 0 else "fC1"
            self.act(lambda e, fo=fo, b=b, thl=thl: e.activation(out=thl, in_=self.ps[b][:, 0:T], func=AF.Tanh, scale=0.5,
                                                                 bias=self.hgb[:, fo:fo + 1]), self.psk[b] + ["hgb"], [tk_])
            self.dve(lambda e, fo=fo, thl=thl: e.scalar_tensor_tensor(out=bo[:, fo, :], in0=thl, scalar=1.0, in1=zbf[:, fo, :],
                                                                      op0=ALU.add, op1=ALU.mult), [tk_, f"zbf{fo}"], [f"bo{fo}"])
        Wbp, rkp = self.wbp_sb, ["wbp"]
        Wbs, rkb = self.wbs_sb, ["wbs"]
        th0 = self.fC[:, 0, 0:T]
        th1 = self.fC[:, 1, 0:T]
        m0 = self.fC[:, 2, 0:T]
        for jp in range(4):
            ring, rk = self.next_w(f"win{2 + jp}")
            W = ring.rearrange("p (k n) -> p k n", k=8)
            for jj in range(2):
                j = 2 * jp + jj
                for br in range(2):
                    col = (br * 2 + jj) * 128
                    b = self.bank()
                    def mm(e, col=col, b=b, W=W):
                        for kc in range(8):
                            ins = e.matmul(self.ps[b][:, 0:T], lhsT=W[:, kc, col:col + 128], rhs=xnT[:, kc, 0:T],
                                           start=(kc == 0), stop=(kc == 7))
                        return ins
                    self.pe(mm, rk + xnk, self.psk[b])
                    tht = th0 if br == 0 else th1
                    tk = "fC0" if br == 0 else "fC1"
                    self.act(lambda e, b=b, tht=tht: e.activation(out=tht, in_=self.ps[b][:, 0:T], func=AF.Tanh, scale=0.5),
                             self.psk[b], [tk])
                    b2 = self.bank()
                    Wb = Wbp if br == 0 else Wbs
                    src4 = aout if br == 0 else bo
                    def bmm(e, b2=b2, Wb=Wb, src4=src4, j=j):
                        for kc in range(4):
                            ins = e.matmul(self.ps[b2][:, 0:T], lhsT=Wb[:, kc, j * 128:(j + 1) * 128], rhs=src4[:, kc, :],
                                           start=(kc == 0), stop=(kc == 3))
                        return ins
                    self.pe(bmm, (rkp if br == 0 else rkb) + [f"{'aout' if br == 0 else 'bo'}{g}" for g in range(4)],
                            self.psk[b2])
                    if br == 0:
                        self.dve(lambda e, b2=b2: e.scalar_tensor_tensor(out=m0, in0=th0, scalar=1.0, in1=self.ps[b2][:, 0:T],
                                                                         op0=ALU.add, op1=ALU.mult),
                                 ["fC0"] + self.psk[b2], ["fC2"])
                    else:
                        self.dve(lambda e, b2=b2: e.scalar_tensor_tensor(out=th1, in0=th1, scalar=1.0, in1=self.ps[b2][:, 0:T],
                                                                         op0=ALU.add, op1=ALU.mult),
                                 ["fC1"] + self.psk[b2], ["fC1"])
                        self.dve(lambda e, j=j: e.scalar_tensor_tensor(out=mbf[:, j, :], in0=th1, scalar=0.5, in1=m0,
                                                                       op0=ALU.mult, op1=ALU.add),
                                 ["fC1", "fC2"], [f"mbf{j}"])
        yield
        self.bpool = "all"
        Wo = []
        for half in range(2):
            ring, rk = self.next_w(f"wout{half}", nopref=True)
            Wo.append((ring.rearrange("p (k n) -> p k n", k=8), rk))
        for nb in range(NB):
            for half in range(2):
                W, rk = Wo[half]
                b = self.bank()
                def omm(e, nb=nb, b=b, W=W):
                    for kc in range(8):
                        ins = e.matmul(self.ps[b][0:npart, :], lhsT=mbf[:, kc, nb * 128: nb * 128 + npart], rhs=W[:, kc, :],
                                       start=(kc == 0), stop=(kc == 7))
                    return ins
                self.pe(omm, rk + [f"mbf{j}" for j in range(8)], self.psk[b])
                hv = xt[0:npart, nb, half * 512:(half + 1) * 512]
                self.dve(lambda e, b=b, hv=hv: e.scalar_tensor_tensor(out=hv, in0=self.ps[b][0:npart, :], scalar=0.5, in1=hv,
                                                                      op0=ALU.mult, op1=ALU.add),
                         self.psk[b] + [xk[nb]], [xk[nb]])
        if self.stream_order is not None:
            depth = 5 if (self.i_last0 is not None and self.stream_pos - 1 >= self.i_last0) else self.NRING
            self.prefetch(self.stream_pos - 1 + depth)
        yield
        self.sumsq(xt, xk, NB, npart, 1)
        yield
        c.hnk = self.norm_T(xt, xk, T, self.gn, "gn", self.hnT, "hnT", 1, do_sumsq=False)
        yield

    def stageF(self, c):
        S = self.S
        k, kind, t0, T, last, NB, npart = c.k, c.kind, c.t0, c.T, c.last, c.NB, c.npart
        xt, xk, fbf, hnk = c.xt, c.xk, c.fbf, c.hnk
        hnT = self.hnT
        S.alias(c.fkeys, c.b1keys)
        sl0 = self.fC[:, 0, 0:T]
        sl1 = self.fC[:, 1, 0:T]
        for j in range(11):
            self.bpool = "B" if j < 3 else "all"
            ring, rk = self.next_w(f"gu{j}")
            Wg = ring[:, 0:2048].rearrange("p (k n) -> p k n", k=8)
            Wu = ring[:, 2048:4096].rearrange("p (k n) -> p k n", k=8)
            for o in range(2):
                fc = 2 * j + o
                bg, bu = self.bank(), self.bank()
                def gm_(e, o=o, bg=bg, Wg=Wg):
                    for kc in range(8):
                        ins = e.matmul(self.ps[bg][:, 0:T], lhsT=Wg[:, kc, o * 128:(o + 1) * 128], rhs=hnT[:, kc, 0:T],
                                       start=(kc == 0), stop=(kc == 7))
                    return ins
                def um_(e, o=o, bu=bu, Wu=Wu):
                    for kc in range(8):
                        ins = e.matmul(self.ps[bu][:, 0:T], lhsT=Wu[:, kc, o * 128:(o + 1) * 128], rhs=hnT[:, kc, 0:T],
                                       start=(kc == 0), stop=(kc == 7))
                    return ins
                self.pe(gm_, rk + hnk, self.psk[bg])
                self.pe(um_, rk + hnk, self.psk[bu])
                sl = sl0 if fc % 2 == 0 else sl1
                sk = "fC0" if fc % 2 == 0 else "fC1"
                self.act(lambda e, bg=bg, sl=sl: e.activation(out=sl, in_=self.ps[bg][:, 0:T], func=AF.Silu),
                         self.psk[bg], [sk])
                self.dve(lambda e, bu=bu, sl=sl, fc=fc: e.tensor_tensor(out=fbf[:, fc, :], in0=sl, in1=self.ps[bu][:, 0:T],
                                                                       op=ALU.mult), [sk] + self.psk[bu], [f"f{fc}"])
            yield
        for p0 in range(0, NB, 2):
            self.bpool = "B"
            nbs = list(range(p0, min(p0 + 2, NB)))
            accs = [(nb, half) for nb in nbs for half in range(2)]
            abank = {a: self.bank() for a in accs}
            used = sorted(set(abank.values()))
            for f0 in range(0, NFC, 4):
                nf = min(4, NFC - f0)
                ring, rk = self.next_w(f"wd{f0}")
                W = ring[:, 0:nf * 1024].rearrange("p (f n) -> p f n", f=nf)
                def dmm(e, f0=f0, nf=nf, W=W, accs=accs, abank=abank):
                    for fi in range(nf):
                        fc = f0 + fi
                        for (nb, half) in accs:
                            ins = e.matmul(self.ps[abank[(nb, half)]][0:npart, :], lhsT=fbf[:, fc, nb * 128: nb * 128 + npart],
                                           rhs=W[:, fi, half * 512:(half + 1) * 512], start=(fc == 0), stop=(fc == NFC - 1))
                    return ins
                self.pe(dmm, rk + [f"f{f0 + fi}" for fi in range(nf)], sum([self.psk[b] for b in used], []))
                yield
                self.bpool = "B"
            for (nb, half) in accs:
                b = abank[(nb, half)]
                hv = xt[0:npart, nb, half * 512:(half + 1) * 512]
                self.dve(lambda e, b=b, hv=hv: e.tensor_tensor(out=hv, in0=self.ps[b][0:npart, :], in1=hv, op=ALU.add),
                         self.psk[b] + [xk[nb]], [xk[nb]])
        self.sumsq(xt, xk, NB, npart, 1)
        dst = self.yp.ap() if kind == "p" else self.ys.ap()
        for nb in range(NB):
            self.dve(lambda e, nb=nb: e.scalar_tensor_tensor(out=xt[0:npart, nb, :], in0=xt[0:npart, nb, :],
                                                             scalar=self.rs[1][0:npart, nb:nb + 1], in1=self.gf[0:npart],
                                                             op0=ALU.mult, op1=ALU.mult), [xk[nb], "rs1", "gf"], [xk[nb]])
            self.store(dst[t0 + nb * 128: t0 + nb * 128 + npart, :], xt[0:npart, nb, :], reads=[xk[nb]])
        yield

    def sample_hist(self, E):
        hb = self.fX
        sp = self.spool.ap().rearrange("s h c -> (s h) c")
        for blk in range(2):
            self.load(hb[0:120, blk * 512:(blk + 1) * 512], sp[blk * 120:(blk + 1) * 120, :], writes=["X2"])
        for g in range(4):
            b = self.bank()
            def tr(e, g=g, b=b):
                for blk in range(2):
                    ins = e.transpose(out=self.ps[b][:, blk * 120:(blk + 1) * 120],
                                      in_=hb[0:120, blk * 512 + g * 128: blk * 512 + (g + 1) * 128],
                                      identity=self.ident_f[0:120, 0:120])
                return ins
            self.pe(tr, ["X2", "ident_f"], self.psk[b])
            self.act(lambda e, g=g, b=b: e.activation(out=E[:, g, :, 0:HIST],
                                                      in_=self.ps[b][:, 0:240].rearrange("p (s h) -> p s h", h=HIST),
                                                      func=AF.Copy), self.psk[b], ["E"])
        t = self.S.op("pool", lambda e: e.dma_start(out=self.o_pool_s.ap()[:, 0:HIST - TS, :],
                                                    in_=self.spool.ap()[:, TS:HIST, :]), dma=True)
        self.final.append(t)

    def sample_state_in(self, S0):
        for ri, src in enumerate((self.sre, self.sim)):
            sin = self.sst[ri]
            self.load(sin, src.ap(), writes=self.sstk[ri])
            b = self.bank()
            def tr(e, sin=sin, b=b):
                for st in range(16):
                    ins = e.transpose(out=self.ps[b][:, st * NS:(st + 1) * NS], in_=sin[0:NS, st * 128:(st + 1) * 128],
                                      identity=self.ident_f[0:NS, 0:NS])
                return ins
            self.pe(tr, self.sstk[ri] + ["ident_f"], self.psk[b])
            self.act(lambda e, ri=ri, b=b: e.activation(out=S0[:, :, ri, :].rearrange("p c s -> p s c"),
                                                        in_=self.ps[b][:, 0:16 * NS].rearrange("p (s c) -> p s c", c=NS),
                                                        func=AF.Copy), self.psk[b], ["S2"])

    def sample_state_out(self, S1):
        for ri, dst in enumerate((self.o_re_s, self.o_im_s)):
            so = self.sst[ri]
            for h in range(4):
                b = self.bank()
                def tr(e, b=b, h=h, ri=ri):
                    for s4 in range(4):
                        st = 4 * h + s4
                        ins = e.transpose(out=self.ps[b][0:NS, s4 * 128:(s4 + 1) * 128], in_=S1[:, :, ri, st],
                                          identity=self.ident_f)
                    return ins
                self.pe(tr, ["S2", "ident_f"], self.psk[b])
                self.act(lambda e, b=b, h=h, so=so: e.activation(out=so[0:NS, h * 512:(h + 1) * 512], in_=self.ps[b][0:NS, :],
                                                                 func=AF.Copy), self.psk[b], self.sstk[ri])
            self.store(dst.ap(), so[0:NS, :], reads=self.sstk[ri])

    def prompt_state_out(self):
        b = self.bank()
        cf = self.carry.rearrange("p r s -> p (r s)")
        self.pe(lambda e, b=b: e.transpose(out=self.ps[b][0:32, 0:128], in_=cf, identity=self.ident_f), ["carry", "ident_f"],
                self.psk[b])
        so = self.msm.rearrange("p a b -> p (a b)")[0:32, 0:128]
        self.act(lambda e, b=b: e.activation(out=so, in_=self.ps[b][0:32, 0:128], func=AF.Copy), self.psk[b], ["msm"])
        self.store(self.o_re_p.ap(), so[0:16, :], reads=["msm"])
        self.store(self.o_im_p.ap(), so[16:32, :], reads=["msm"])

    def build(self, stage=None):
        import os
        stage = stage or os.environ.get("KSTAGE", "all")
        self.setup_consts()
        if stage == "consts":
            self.S.emit_all(self.final); return self.nc
        self.convert_weights()
        if stage == "convert":
            self.final += [self.S.last_w[k] for k in self.S.last_w if k.startswith("s_")]
            self.S.emit_all(self.final); return self.nc
        tiles = [("p", i * TP, TP) for i in range(SEQ // TP)] + [("s", 0, NS * TS)]
        if stage.startswith("tiles"):
            tiles = tiles[:int(stage[5:])]
        ctxs = [self.mkctx(k, kind, t0, T, last=(kind == "p" and t0 + T == SEQ)) for k, (kind, t0, T) in enumerate(tiles)]

        def run(g):
            for _ in g:
                pass

        self.setup_s5_p1()
        self.load(self.gm, self.norm_mix.ap().rearrange("(k p) -> p k", p=128), writes=["gm"], slow=True)
        self.prefetch(self.NRING)
        a0 = self.stageA(ctxs[0])
        next(a0)
        next(a0)
        k1 = (["E", "X2", "PT", "S2", "Sbf"] + [f"diff{g}" for g in range(4)] + [f"uT{g}" for g in range(4)]
              + [f"aout{g}" for g in range(4)])
        self.S.alias(k1, self.setup_keys[1])
        self.dve(lambda e: e.memset(self.fB, 0.0), [], ["PT"])
        self.load_small_vectors()
        p2 = self.setup_s5_p2()
        next(p2)
        self.load_resident()
        for _ in range(4):
            next(a0)
        next(p2)
        cvg = self.convert_weights_ffn()
        ca, aa = True, True
        while ca or aa:
            if ca:
                try:
                    next(cvg)
                except StopIteration:
                    ca = False
            if aa:
                try:
                    next(a0)
                except StopIteration:
                    aa = False
        run(p2)
        k2 = ([f"xt1_{nb}" for nb in range(4)] + ["fC0", "fC1", "fC2"] + [f"hnT{h}_{nb}" for h in range(2) for nb in range(4)]
              + [f"zbf{g}" for g in range(4)] + [f"bo{g}" for g in range(4)] + [f"mbf{j}" for j in range(8)]
              + [f"f{c}" for c in range(NFC)])
        self.S.alias(k2, self.setup_keys[2])
        print("sbuf bytes remaining:", self.nc.sbuf_bytes_remaining, "scratch", {p: (s["i"], s["off"]) for p, s in self.scr_state.items()})
        for k in range(len(ctxs)):
            a = self.stageA(ctxs[k + 1]) if k + 1 < len(ctxs) else None
            if a is not None:
                self.load_x(ctxs[k + 1])
            b1 = self.stageB1(ctxs[k])
            next(b1)
            if a is not None:
                next(a)
            next(b1)
            next(b1)
            if a is not None:
                next(a)
                next(a)
            run(b1)
            f = self.stageF(ctxs[k])
            fa, aa = True, a is not None
            while fa or aa:
                if fa:
                    try:
                        next(f)
                    except StopIteration:
                        fa = False
                if aa:
                    try:
                        next(a)
                    except StopIteration:
                        aa = False
        if self.stream_order is None:
            return self.rec, self.sample_first_idx
        self.S.emit_all(self.final)
        return self.nc


def build_program():
    rec, i_last0 = Builder().build()
    return Builder(stream_order=rec, i_last0=i_last0).build()


def perm_w_in(w):
    cols = [np.arange(0, 1024)]
    for jp in range(4):
        cols.append(np.arange(1024 + 256 * jp, 1024 + 256 * (jp + 1)))
        cols.append(np.arange(2048 + 256 * jp, 2048 + 256 * (jp + 1)))
    return np.ascontiguousarray(w[:, np.concatenate(cols)])


def make_in_maps(inp):
    f = lambda a: np.ascontiguousarray(np.asarray(a, dtype=np.float32))
    shared = {
        "norm_mix": f(inp["norm_mix"][0]), "w_in": perm_w_in(f(inp["w_in"][0])), "pool_w": f(inp["pool_w"][0]),
        "pool_scale": f(inp["pool_scale"][0]), "ssm_a_re": f(inp["ssm_a_re"][0]).reshape(-1),
        "ssm_a_im": f(inp["ssm_a_im"][0]).reshape(-1), "ssm_log_dt": f(inp["ssm_log_dt"][0]),
        "ssm_b_re": f(inp["ssm_b_re"][0]).reshape(-1), "ssm_b_im": f(inp["ssm_b_im"][0]).reshape(-1),
        "ssm_c_re": f(inp["ssm_c_re"][0]).reshape(-1), "ssm_c_im": f(inp["ssm_c_im"][0]).reshape(-1),
        "ssm_d": f(inp["ssm_d"][0]), "glu_w": f(inp["glu_w"][0]), "glu_b": f(inp["glu_b"][0]),
        "w_branch_pool": f(inp["w_branch_pool"][0]), "w_branch_ssm": f(inp["w_branch_ssm"][0]),
        "w_out": f(inp["w_out"][0]), "norm_ffn": f(inp["norm_ffn"][0]), "ffn_w_gate": f(inp["ffn_w_gate"][0]),
        "ffn_w_up": f(inp["ffn_w_up"][0]), "ffn_w_down": f(inp["ffn_w_down"][0]), "norm_final": f(inp["norm_final"]),
    }
    maps = []
    for b in range(N_CORES):
        m = dict(shared)
        m["xp"] = f(inp["x_prompt"][b])
        m["xs"] = f(np.transpose(np.asarray(inp["x_sample"])[NS * b:NS * (b + 1)], (1, 0, 2)).reshape(NS * TS, D))
        m["spool"] = f(inp["state_pool"][0, NS * b:NS * (b + 1)])
        m["sre"] = f(np.asarray(inp["state_ssm_re"])[0, NS * b:NS * (b + 1)].reshape(NS, 2048))
        m["sim"] = f(np.asarray(inp["state_ssm_im"])[0, NS * b:NS * (b + 1)].reshape(NS, 2048))
        maps.append(m)
    return maps


_CACHE = {}


def kernel(**inp):
    if "nc" not in _CACHE:
        _CACHE["nc"] = build_program()
    nc = _CACHE["nc"]
    maps = make_in_maps(inp)
    res = run_bass_kernel_spmd(nc, maps, core_ids=list(range(N_CORES)))
    R = res.results
    y_p = np.stack([R[b]["yp"] for b in range(N_CORES)])
    y_s = np.concatenate([R[b]["ys"].reshape(TS, NS, D).transpose(1, 0, 2) for b in range(N_CORES)])
    pool_p = np.stack([R[b]["o_pool_p"] for b in range(N_CORES)])[None]
    re_p = np.stack([R[b]["o_re_p"].reshape(32, 64) for b in range(N_CORES)])[None]
    im_p = np.stack([R[b]["o_im_p"].reshape(32, 64) for b in range(N_CORES)])[None]
    pool_s = np.concatenate([R[b]["o_pool_s"] for b in range(N_CORES)])[None]
    re_s = np.concatenate([R[b]["o_re_s"].reshape(NS, 32, 64) for b in range(N_CORES)])[None]
    im_s = np.concatenate([R[b]["o_im_s"].reshape(NS, 32, 64) for b in range(N_CORES)])[None]
    out = (y_p, y_s, pool_p, re_p, im_p, pool_s, re_s, im_s)
    return tuple(np.ascontiguousarray(o, dtype=np.float32) for o in out)
```
